# Optimizing a Trainium2 kernel written in Bass

```python
import jax, jax.numpy as jnp
from jax import lax
import numpy as np

D_MODEL = 4096
BATCH = 1
SEQ = 8192
DEPTH = 4

EXPAND = 2
D_INNER = EXPAND * D_MODEL
HEAD_DIM = 128
N_HEADS = D_INNER // HEAD_DIM
GROUP_WIDTH = 128
N_GROUPS = D_INNER // GROUP_WIDTH
CHUNK = 128
Q_BLOCK = 128
CONV_WIDTH = 3
N_MIXERS = 3
RMS_EPS = 1e-6
NEG_INF = -1e30

kernel_name = "interleaved_sgu_fox_shortconv_trunk"


def rmsnorm(x, gain):
    xf = x.astype(jnp.float32)
    y = xf * lax.rsqrt(jnp.mean(xf * xf, axis=-1, keepdims=True) + RMS_EPS)
    return (y * gain.astype(jnp.float32)).astype(x.dtype)


def mixer_a(h, w_in, v_gain, ws, ws_bias):
    b, l, _ = h.shape
    u, v, g = jnp.split(h @ w_in, 3, axis=-1)
    v = rmsnorm(v, v_gain)
    v = v.reshape(b, l // CHUNK, CHUNK, N_GROUPS, GROUP_WIDTH)
    causal = jnp.tril(jnp.ones((CHUNK, CHUNK), dtype=bool))
    ws_c = jnp.where(causal[None], ws, jnp.zeros((), ws.dtype)).astype(v.dtype)
    s = jnp.einsum('gts,bnsgc->bntgc', ws_c, v) + ws_bias.T.astype(v.dtype)[:, :, None]
    s = s.reshape(b, l, D_INNER)
    return u * s * jax.nn.silu(g)


def mixer_b(h, w_in, b_f):
    b, l, _ = h.shape
    z = h @ w_in
    q, k, v, g = jnp.split(z[..., :4 * D_INNER], 4, axis=-1)
    f_logit = z[..., 4 * D_INNER:]
    log_f = jax.nn.log_sigmoid(f_logit.astype(jnp.float32) + b_f.astype(jnp.float32))
    cum = jnp.swapaxes(jnp.cumsum(log_f, axis=1), 1, 2)
    q = q.reshape(b, l, N_HEADS, HEAD_DIM)
    k = k.reshape(b, l, N_HEADS, HEAD_DIM)
    v = v.reshape(b, l, N_HEADS, HEAD_DIM)
    scale = HEAD_DIM ** -0.5
    key_pos = jnp.arange(l)

    def attend_block(i):
        start = i * Q_BLOCK
        qb = lax.dynamic_slice_in_dim(q, start, Q_BLOCK, axis=1)
        cq = lax.dynamic_slice_in_dim(cum, start, Q_BLOCK, axis=2)
        s = jnp.einsum('bqhd,bkhd->bhqk', qb, k).astype(jnp.float32) * scale
        s = s + cq[..., :, None] - cum[..., None, :]
        q_pos = start + jnp.arange(Q_BLOCK)
        s = jnp.where(q_pos[:, None] >= key_pos[None, :], s, NEG_INF)
        p = jax.nn.softmax(s, axis=-1).astype(v.dtype)
        return jnp.einsum('bhqk,bkhd->bqhd', p, v)

    o = lax.map(attend_block, jnp.arange(l // Q_BLOCK))
    o = jnp.moveaxis(o, 0, 1).reshape(b, l, D_INNER)
    return o * jax.nn.silu(g)


def mixer_c(h, w_in, conv_w):
    bg, cg, hh, g = jnp.split(h @ w_in, 4, axis=-1)
    inner = cg * hh
    conv = lax.conv_general_dilated(
        inner, conv_w[:, None, :].astype(inner.dtype),
        window_strides=(1,), padding=((CONV_WIDTH - 1, 0),),
        dimension_numbers=('NWC', 'WIO', 'NWC'), feature_group_count=D_INNER)
    return bg * conv * jax.nn.silu(g)


def setup_inputs(seed: int = 0) -> dict:
    key = jax.random.key(seed)
    keys = jax.random.split(key, 40)
    f32 = jnp.float32
    d_s = D_MODEL ** -0.5
    e_s = D_INNER ** -0.5

    def gain(k, n):
        return 1.0 + 0.02 * jax.random.normal(k, (n,), f32)

    def layer_a(ks):
        return (gain(ks[0], D_MODEL),
                jax.random.normal(ks[1], (D_MODEL, 3 * D_INNER), f32) * d_s,
                gain(ks[2], D_INNER),
                jax.random.normal(ks[3], (N_GROUPS, CHUNK, CHUNK), f32) * CHUNK ** -0.5,
                1.0 + 0.02 * jax.random.normal(ks[4], (N_GROUPS, CHUNK), f32),
                jax.random.normal(ks[5], (D_INNER, D_MODEL), f32) * e_s)

    l0 = layer_a(keys[0:6])
    l3 = layer_a(keys[6:12])
    l1_norm = gain(keys[12], D_MODEL)
    l1_w_in = jax.random.normal(keys[13], (D_MODEL, 4 * D_INNER + N_HEADS), f32) * d_s
    l1_b_f = jnp.linspace(2.0, 7.0, N_HEADS, dtype=f32) + 0.01 * jax.random.normal(keys[14], (N_HEADS,), f32)
    l1_w_out = jax.random.normal(keys[15], (D_INNER, D_MODEL), f32) * e_s
    l2_norm = gain(keys[16], D_MODEL)
    l2_w_in = jax.random.normal(keys[17], (D_MODEL, 4 * D_INNER), f32) * d_s
    l2_conv_w = jax.random.normal(keys[18], (CONV_WIDTH, D_INNER), f32) * CONV_WIDTH ** -0.5
    l2_w_out = jax.random.normal(keys[19], (D_INNER, D_MODEL), f32) * e_s
    x = jax.random.normal(keys[20], (BATCH, SEQ, D_MODEL), f32)
    final_norm = gain(keys[21], D_MODEL)
    return {
        "x": x,
        "l0_norm": l0[0], "l0_w_in": l0[1], "l0_v_gain": l0[2], "l0_ws": l0[3], "l0_ws_bias": l0[4], "l0_w_out": l0[5],
        "l1_norm": l1_norm, "l1_w_in": l1_w_in, "l1_b_f": l1_b_f, "l1_w_out": l1_w_out,
        "l2_norm": l2_norm, "l2_w_in": l2_w_in, "l2_conv_w": l2_conv_w, "l2_w_out": l2_w_out,
        "l3_norm": l3[0], "l3_w_in": l3[1], "l3_v_gain": l3[2], "l3_ws": l3[3], "l3_ws_bias": l3[4], "l3_w_out": l3[5],
        "final_norm": final_norm,
    }


def reference(x,
              l0_norm, l0_w_in, l0_v_gain, l0_ws, l0_ws_bias, l0_w_out,
              l1_norm, l1_w_in, l1_b_f, l1_w_out,
              l2_norm, l2_w_in, l2_conv_w, l2_w_out,
              l3_norm, l3_w_in, l3_v_gain, l3_ws, l3_ws_bias, l3_w_out,
              final_norm):
    layers = (
        (l0_norm, l0_w_in, (l0_v_gain, l0_ws, l0_ws_bias), l0_w_out),
        (l1_norm, l1_w_in, (l1_b_f,), l1_w_out),
        (l2_norm, l2_w_in, (l2_conv_w,), l2_w_out),
        (l3_norm, l3_w_in, (l3_v_gain, l3_ws, l3_ws_bias), l3_w_out),
    )
    mixers = (mixer_a, mixer_b, mixer_c)
    for i in range(DEPTH):
        norm_g, w_in, extra, w_out = layers[i]
        y = mixers[i % N_MIXERS](rmsnorm(x, norm_g), w_in, *extra)
        x = x + y @ w_out
    return rmsnorm(x, final_norm)
```

```python
import numpy as np
import ml_dtypes
from contextlib import ExitStack
import concourse.bass as bass
import concourse.mybir as mybir
from concourse.bass_utils import run_bass_kernel_spmd

F32 = mybir.dt.float32
BF16 = mybir.dt.bfloat16
AF = mybir.ActivationFunctionType
ALU = mybir.AluOpType
NCORES = 8
RMS_EPS = 1e-6
TB = 512


class Cfg:
    def __init__(self, L=8192, D=4096, E=8192):
        self.L, self.D, self.E = L, D, E
        self.FC = D // NCORES // 128
        self.FCR = self.FC * 128
        self.HPC = E // NCORES // 128
        self.EC = self.HPC * 128
        self.KD = D // 128
        self.KE = E // 128
        self.NTB = L // TB
        self.NTT = L // 128


MIX = ("A", "B", "C", "A")


class Buf:
    __slots__ = ("name", "w", "r", "dsem")

    def __init__(self, name):
        self.name = name
        self.w = None
        self.r = {}
        self.dsem = None


class Sched:
    ENGS = ("pe", "act", "dve", "pool", "sp")

    def __init__(self, nc, stack):
        self.nc = nc
        self.stack = stack
        self.lists = {e: [] for e in self.ENGS}
        self.semh = {}
        self.semcnt = {}
        self.seen = {e: {} for e in self.ENGS}
        self.esem = {}
        for e in ("pe", "act", "dve", "pool"):
            self.esem[e] = self.new_sem("e_" + e)
        self.nsem = 0

    def new_sem(self, name):
        key = name + "_%d" % len(self.semh)
        self.semh[key] = self.stack.enter_context(self.nc.semaphore(key))
        self.semcnt[key] = 0
        return key

    def _waits(self, eng, toks):
        seen = self.seen[eng]
        own = self.esem.get(eng)
        for (s, v) in toks:
            if eng == "pe" and s == own:
                continue
            if seen.get(s, 0) >= v:
                continue
            seen[s] = v
            h = self.semh[s]
            self.lists[eng].append(lambda e, h=h, v=v: e.wait_ge(h, v))

    @staticmethod
    def _deps(reads, writes, deps):
        toks = list(deps)
        for b in reads:
            if b.w is not None:
                toks.append(b.w)
        for b in writes:
            if b.w is not None:
                toks.append(b.w)
            toks.extend(b.r.items())
        return toks

    @staticmethod
    def _mark(tok, reads, writes):
        s, v = tok
        for b in reads:
            if b.r.get(s, 0) < v:
                b.r[s] = v
        for b in writes:
            b.w = tok
            b.r = {}

    def op(self, eng, fn, reads=(), writes=(), deps=()):
        self._waits(eng, self._deps(reads, writes, deps))
        s = self.esem[eng]
        self.semcnt[s] += 1
        tok = (s, self.semcnt[s])
        h = self.semh[s]
        self.lists[eng].append(lambda e, fn=fn, h=h: fn(e).then_inc(h, 1))
        self._mark(tok, reads, writes)
        return tok

    def group(self, eng, fns, reads=(), writes=(), deps=()):
        self._waits(eng, self._deps(reads, writes, deps))
        s = self.esem[eng]
        self.semcnt[s] += 1
        tok = (s, self.semcnt[s])
        h = self.semh[s]
        lst = self.lists[eng]
        for fn in fns[:-1]:
            lst.append(fn)
        lst.append(lambda e, fn=fns[-1], h=h: fn(e).then_inc(h, 1))
        self._mark(tok, reads, writes)
        return tok

    def dma(self, q, out_ap, in_ap, reads=(), writes=(), deps=(), sembuf=None):
        sb = sembuf if sembuf is not None else writes[0]
        if sb.dsem is None:
            sb.dsem = self.new_sem("d_" + sb.name)
        s = sb.dsem
        self._waits(q, self._deps(reads, writes, deps))
        self.semcnt[s] += 16
        tok = (s, self.semcnt[s])
        h = self.semh[s]
        self.lists[q].append(lambda e, o=out_ap, i=in_ap, h=h: e.dma_start(out=o, in_=i).then_inc(h, 16))
        self._mark(tok, reads, writes)
        return tok

    def collective(self, kind, op, in_ap, out_ap, reads=(), writes=()):
        import os
        if os.environ.get("NOCC"):
            n = in_ap.shape[0]
            return self.dma("sp", out_ap[0:n, :], in_ap, reads=reads, writes=writes)
        s = self.new_sem("cc")
        self._waits("pool", self._deps(reads, writes, ()))
        self.semcnt[s] = 1
        tok = (s, 1)
        h = self.semh[s]
        self.lists["pool"].append(
            lambda e, h=h: e.collective_compute(kind, op, replica_groups=[list(range(NCORES))],
                                                ins=[in_ap], outs=[out_ap]).then_inc(h, 1))
        self._mark(tok, reads, writes)
        self._waits("pool", [tok])
        return tok

    def barrier(self):
        toks = [(s, v) for s, v in self.semcnt.items() if v > 0]
        for e in self.ENGS:
            self._waits(e, toks)

    def emit(self, block):
        nc = self.nc
        L = self.lists

        @block.tensor
        def _(e):
            for f in L["pe"]:
                f(e)

        @block.scalar
        def _(e):
            for f in L["act"]:
                f(e)

        @block.vector
        def _(e):
            for f in L["dve"]:
                f(e)

        @block.gpsimd
        def _(e):
            for f in L["pool"]:
                f(e)

        @block.sync
        def _(e):
            for f in L["sp"]:
                f(e)


class Arena:
    def __init__(self, t, words):
        self.t = t
        self.words = words
        self.off = 0

    def reset(self):
        self.off = 0

    def f32(self, n):
        assert self.off + n <= self.words, ("SBUF arena overflow", self.off, n, self.words)
        ap = self.t[:, self.off:self.off + n]
        self.off += n
        return ap

    def bf16(self, n):
        w = (n + 1) // 2
        return self.f32(w).bitcast(BF16)


def build_segment(cfg, kind):
    L, D, E = cfg.L, cfg.D, cfg.E
    FC, FCR, HPC, EC, KD, KE, NTB, NTT = cfg.FC, cfg.FCR, cfg.HPC, cfg.EC, cfg.KD, cfg.KE, cfg.NTB, cfg.NTT
    nc = bass.Bass("TRN2", target_bir_lowering=False)
    stack = ExitStack()
    S = Sched(nc, stack)

    def ext_in(name, shape, dt=F32):
        return nc.dram_tensor(name, list(shape), dt, kind="ExternalInput").ap()

    def ext_out(name, shape, dt=F32):
        return nc.dram_tensor(name, list(shape), dt, kind="ExternalOutput").ap()

    cmask_in = ext_in("cmask", [128, 128])
    ident_in = ext_in("ident", [128, 128])
    p = {}
    xres = xout = hT_part = hT_full = yT_part = yT_full = ss_part = ss_all = ssv_part = ssv_all = vraw = out_ext = None
    if kind == "ss0":
        xres = ext_in("xT", [FCR, L])
        ss_part = ext_out("ss_part", [1, L])
    elif kind in ("norm", "final"):
        xres = ext_in("xT", [FCR, L])
        ss_all = ext_in("ss_all", [NCORES, L])
        p["norm"] = ext_in("gain", [128, FC])
        if kind == "norm":
            hT_part = ext_out("hT_part", [FCR, L], BF16)
        else:
            out_ext = ext_out("out", [L, FCR])
    elif kind.startswith("mix"):
        hT_full = ext_in("hT_full", [D, L], BF16)
        m = kind[3]
        p["win"] = ext_in("win", [D, 3 * EC if m == "A" else 4 * EC])
        if kind == "mixA1":
            ssv_part = ext_out("ssv_part", [128, NTT])
            vraw = ext_out("vraw", [L, EC], BF16)
        else:
            yT_part = ext_out("yT_part", [EC, L], BF16)
        if kind == "mixA2":
            ssv_all = ext_in("ssv_all", [NCORES * 128, NTT])
            vraw = ext_in("vraw", [L, EC], BF16)
            p["vgain"] = ext_in("vgain", [128, HPC])
            p["wsT"] = ext_in("wsT", [128, HPC * 128])
            p["wsb"] = ext_in("wsb", [1, HPC * 128])
        elif kind == "mixB":
            p["wf"] = ext_in("wf", [D, HPC])
            p["bf"] = ext_in("bf", [1, HPC])
        elif kind == "mixC":
            p["convw"] = ext_in("convw", [128, HPC * 3])
    elif kind == "out":
        yT_full = ext_in("yT_full", [E, L], BF16)
        xres = ext_in("xT", [FCR, L])
        p["wout"] = ext_in("wout", [E, FCR])
        xout = ext_out("xT_out", [FCR, L])
        ss_part = ext_out("ss_part", [1, L])
    else:
        raise ValueError(kind)

    B_xres = [Buf(f"xres{tb}") for tb in range(NTB)]
    B_xout = [Buf(f"xout{tb}") for tb in range(NTB)]
    B_hT_part, B_hT_full = Buf("hT_part"), Buf("hT_full")
    B_yT_part, B_yT_full = Buf("yT_part"), Buf("yT_full")
    B_ss_part, B_ss_full = Buf("ss_part"), Buf("ss_full")
    B_ssv_part, B_ssv_full = Buf("ssv_part"), Buf("ssv_full")
    B_vraw = Buf("vraw")

    ARENA_WORDS = 51200
    arena_t = stack.enter_context(nc.sbuf_tensor("arena", [128, ARENA_WORDS], F32))
    const_t = stack.enter_context(nc.sbuf_tensor("consts", [128, 1024], F32))
    AR = Arena(arena_t, ARENA_WORDS)
    psum = [stack.enter_context(nc.psum_tensor(f"ps{i}", [128, 512], F32)) for i in range(8)]
    B_ps = [Buf(f"ps{i}") for i in range(8)]

    cmask_f = const_t[:, 0:128]
    ident_f = const_t[:, 128:256]
    ones_f = const_t[:, 256:384]
    cmask_b = const_t[:, 384:448].bitcast(BF16)
    ones_b = const_t[:, 448:512].bitcast(BF16)
    B_const = Buf("const")
    S.dma("sp", cmask_f, cmask_in, writes=[B_const])
    S.dma("sp", ident_f, ident_in, writes=[B_const])
    S.op("dve", lambda e: e.memset(ones_f, 1.0), writes=[B_const])
    S.op("dve", lambda e: e.memset(ones_b, 1.0), writes=[B_const])
    S.op("dve", lambda e: e.tensor_copy(out=cmask_b, in_=cmask_f), reads=[B_const], writes=[B_const])
    S.barrier()

    def xblk(src, tb):
        return src[:, tb * TB:(tb + 1) * TB].rearrange("(j p) t -> p j t", p=128)

    def sumsq_pass(src, B_src):
        AR.reset()
        xb = [AR.f32(FC * TB).rearrange("p (j t) -> p j t", j=FC) for _ in range(2)]
        sq = [AR.f32(FC * TB).rearrange("p (j t) -> p j t", j=FC) for _ in range(2)]
        ssb = [AR.f32(TB) for _ in range(2)]
        B_xb = [Buf("sxb0"), Buf("sxb1")]
        B_sq = [Buf("sq0"), Buf("sq1")]
        B_ssb = [Buf("ssb0"), Buf("ssb1")]
        for tb in range(NTB):
            p_ = tb % 2
            S.dma("sp", xb[p_], xblk(src, tb), reads=[B_src[tb]], writes=[B_xb[p_]])
            S.op("act", lambda e, p_=p_: e.activation(out=sq[p_], in_=xb[p_], func=AF.Square),
                 reads=[B_xb[p_]], writes=[B_sq[p_]])
            fns = []
            for j in range(FC):
                fns.append(lambda e, p_=p_, j=j: e.matmul(psum[p_][:, :], lhsT=ones_f, rhs=sq[p_][:, j, :],
                                                           start=(j == 0), stop=(j == FC - 1)))
            S.group("pe", fns, reads=[B_sq[p_], B_const], writes=[B_ps[p_]])
            S.op("dve", lambda e, p_=p_: e.tensor_copy(out=ssb[p_][0:1, :], in_=psum[p_][0:1, :]),
                 reads=[B_ps[p_]], writes=[B_ssb[p_]])
            S.dma("sp", ss_part[:, tb * TB:(tb + 1) * TB], ssb[p_][0:1, :], reads=[B_ssb[p_]], writes=[B_ss_part])
        S.barrier()

    def norm_stage(gain_in, to_output):
        AR.reset()
        rstd_t = AR.f32(L)
        rtmp = AR.f32(L)
        gain_t = AR.f32(FC)
        xb = [AR.f32(FC * TB).rearrange("p (j t) -> p j t", j=FC) for _ in range(2)]
        B_rstd, B_gain, B_rtmp = Buf("rstd"), Buf("gain"), Buf("rtmp")
        B_xb = [Buf("xb0"), Buf("xb1")]
        S.dma("sp", gain_t, gain_in, writes=[B_gain])
        S.dma("sp", rstd_t, ss_all[0:1, :].partition_broadcast(128), writes=[B_rstd])
        for r in range(1, NCORES):
            S.dma("sp", rtmp, ss_all[r:r + 1, :].partition_broadcast(128), writes=[B_rtmp])
            S.op("dve", lambda e: e.tensor_tensor(out=rstd_t, in0=rstd_t, in1=rtmp, op=ALU.add),
                 reads=[B_rstd, B_rtmp], writes=[B_rstd])
        S.op("dve", lambda e: e.tensor_scalar(out=rstd_t, in0=rstd_t, scalar1=1.0 / D, scalar2=RMS_EPS,
                                              op0=ALU.mult, op1=ALU.add), reads=[B_rstd], writes=[B_rstd])
        S.op("act", lambda e: e.sqrt(out=rstd_t, in_=rstd_t), reads=[B_rstd], writes=[B_rstd])
        S.op("dve", lambda e: e.reciprocal(out=rstd_t, in_=rstd_t), reads=[B_rstd], writes=[B_rstd])
        if not to_output:
            hb = [AR.bf16(FC * TB).rearrange("p (j t) -> p j t", j=FC) for _ in range(2)]
            B_hb = [Buf("hb0"), Buf("hb1")]
            for tb in range(NTB):
                p = tb % 2
                S.dma("sp", xb[p], xblk(xres, tb), reads=[B_xres[tb]], writes=[B_xb[p]])
                for j in range(FC):
                    S.op("dve", lambda e, p=p, j=j, tb=tb: e.scalar_tensor_tensor(
                        out=hb[p][:, j, :], in0=xb[p][:, j, :], scalar=gain_t[:, j:j + 1],
                        in1=rstd_t[:, tb * TB:(tb + 1) * TB], op0=ALU.mult, op1=ALU.mult),
                        reads=[B_xb[p], B_rstd, B_gain], writes=[B_hb[p]])
                S.dma("sp", xblk(hT_part, tb), hb[p], reads=[B_hb[p]], writes=[B_hT_part])
        else:
            hf = [AR.f32(FC * TB).rearrange("p (j t) -> p j t", j=FC) for _ in range(2)]
            ob = [AR.f32(4 * FCR).rearrange("p (a c) -> p a c", a=4) for _ in range(2)]
            B_hf = [Buf("hf0"), Buf("hf1")]
            B_ob = [Buf("ob0"), Buf("ob1")]
            B_out = Buf("out")
            for tb in range(NTB):
                p = tb % 2
                S.dma("sp", xb[p], xblk(xres, tb), reads=[B_xres[tb]], writes=[B_xb[p]])
                for j in range(FC):
                    S.op("dve", lambda e, p=p, j=j, tb=tb: e.scalar_tensor_tensor(
                        out=hf[p][:, j, :], in0=xb[p][:, j, :], scalar=gain_t[:, j:j + 1],
                        in1=rstd_t[:, tb * TB:(tb + 1) * TB], op0=ALU.mult, op1=ALU.mult),
                        reads=[B_xb[p], B_rstd, B_gain], writes=[B_hf[p]])
                for a in range(4):
                    bk = (tb * 4 + a) % 4
                    fns = []
                    for j in range(FC):
                        fns.append(lambda e, p=p, a=a, j=j, bk=bk: e.transpose(
                            psum[bk][:, j * 128:(j + 1) * 128], hf[p][:, j, a * 128:(a + 1) * 128], ident_f))
                    S.group("pe", fns, reads=[B_hf[p], B_const], writes=[B_ps[bk]])
                    S.op("act", lambda e, p=p, a=a, bk=bk: e.copy(out=ob[p][:, a, :], in_=psum[bk][:, 0:FCR]),
                         reads=[B_ps[bk]], writes=[B_ob[p]])
                S.dma("sp", out_ext[tb * TB:(tb + 1) * TB, :].rearrange("(a p) c -> p a c", p=128), ob[p],
                      reads=[B_ob[p]], writes=[B_out])
        S.barrier()

    def inproj(win, colgroups, on_tile, pre_group=None, post_group=None, extra_tb=None, ps_banks=(0, 1)):
        hT = [AR.bf16(KD * TB).rearrange("p (k t) -> p k t", k=KD) for _ in range(2)]
        B_hT = [Buf("hT0"), Buf("hT1")]
        gwmax = max(g["width"] for g in colgroups)
        Wt = AR.bf16(KD * gwmax)
        KPS = min(4, KD)
        NPIECE = KD // KPS
        stg = [AR.f32(KPS * gwmax) for _ in range(2)]
        B_stg = [Buf("stg0"), Buf("stg1")]
        B_W = [Buf(f"W{q}") for q in range(NPIECE)]
        cnt = 0
        scnt = 0
        for gi, g in enumerate(colgroups):
            gw = g["width"]
            W = Wt[:, 0:KD * gw].rearrange("p (k c) -> p k c", k=KD)
            for q in range(NPIECE):
                sp_ = scnt % 2
                scnt += 1
                st = stg[sp_][:, 0:KPS * gw].rearrange("p (k c) -> p k c", k=KPS)
                src = win[q * KPS * 128:(q + 1) * KPS * 128, g["c0"]:g["c0"] + gw].rearrange("(k p) c -> p k c", p=128)
                S.dma("sp", st, src, writes=[B_stg[sp_]])
                S.op("dve", lambda e, st=st, W=W, q=q: e.tensor_copy(out=W[:, q * KPS:(q + 1) * KPS, :], in_=st),
                     reads=[B_stg[sp_]], writes=[B_W[q]])
            if pre_group is not None:
                pre_group(gi)
            for tb in range(NTB):
                hp = (gi * NTB + tb) % 2
                HQ = 4 if KD % 4 == 0 else 1
                KQ = KD // HQ
                for q in range(HQ):
                    src = hT_full[q * KQ * 128:(q + 1) * KQ * 128, tb * TB:(tb + 1) * TB].rearrange(
                        "(k p) t -> p k t", p=128)
                    S.dma("sp", hT[hp][:, q * KQ:(q + 1) * KQ, :], src, reads=[B_hT_full], writes=[B_hT[hp]])
                if extra_tb is not None and gi == 0:
                    extra_tb(tb, hT[hp], B_hT[hp])
                if g["orient"] == "TW":
                    for a in range(4):
                        bk = ps_banks[cnt % 2]
                        cnt += 1
                        fns = [lambda e, k=k, a=a, bk=bk, hp=hp, W=W, gw=gw: e.matmul(
                            psum[bk][:, 0:gw], lhsT=hT[hp][:, k, a * 128:(a + 1) * 128], rhs=W[:, k, :],
                            start=(k == 0), stop=(k == KD - 1)) for k in range(KD)]
                        S.group("pe", fns, reads=[B_hT[hp]] + B_W, writes=[B_ps[bk]])
                        on_tile(gi, tb, a, bk)
                else:
                    for cc, orient in enumerate(g["orient"]):
                        bk = ps_banks[cnt % 2]
                        cnt += 1
                        if orient == "F":
                            fns = [lambda e, k=k, cc=cc, bk=bk, hp=hp, W=W: e.matmul(
                                psum[bk][:, :], lhsT=W[:, k, cc * 128:(cc + 1) * 128], rhs=hT[hp][:, k, :],
                                start=(k == 0), stop=(k == KD - 1)) for k in range(KD)]
                        else:
                            fns = []
                            for a in range(4):
                                for k in range(KD):
                                    fns.append(lambda e, k=k, a=a, cc=cc, bk=bk, hp=hp, W=W: e.matmul(
                                        psum[bk][:, a * 128:(a + 1) * 128], lhsT=hT[hp][:, k, a * 128:(a + 1) * 128],
                                        rhs=W[:, k, cc * 128:(cc + 1) * 128], start=(k == 0), stop=(k == KD - 1)))
                        S.group("pe", fns, reads=[B_hT[hp]] + B_W, writes=[B_ps[bk]])
                        on_tile(gi, tb, cc, bk)
            if post_group is not None:
                post_group(gi)

    def mixer_C(p):
        AR.reset()
        cw = AR.f32(HPC * 3)
        B_cw = Buf("cw")
        S.dma("sp", cw, p["convw"], writes=[B_cw])
        bg = [AR.f32(TB) for _ in range(2)]
        cg = [AR.f32(TB) for _ in range(2)]
        inner = [AR.f32(TB + 2) for _ in range(2)]
        sg = [AR.f32(TB) for _ in range(2)]
        tt = [AR.f32(TB) for _ in range(2)]
        yb = [AR.bf16(TB) for _ in range(2)]
        B_bg, B_cg, B_in, B_sg, B_tt, B_yb = ([Buf(n + "0"), Buf(n + "1")] for n in ("bg", "cg", "in", "sg", "tt", "yb"))
        colgroups = [dict(c0=j * 512, width=512, orient="FFFF") for j in range(HPC)]

        def on_tile(j, tb, cc, bk):
            q = tb % 2
            if cc == 0:
                S.op("act", lambda e: e.copy(out=bg[q], in_=psum[bk][:, :]), reads=[B_ps[bk]], writes=[B_bg[q]])
            elif cc == 1:
                S.op("act", lambda e: e.copy(out=cg[q], in_=psum[bk][:, :]), reads=[B_ps[bk]], writes=[B_cg[q]])
            elif cc == 2:
                if tb == 0:
                    S.op("dve", lambda e: e.memset(inner[q][:, 0:2], 0.0), writes=[B_in[q]])
                S.op("dve", lambda e: e.tensor_tensor(out=inner[q][:, 2:TB + 2], in0=psum[bk][:, :], in1=cg[q],
                                                      op=ALU.mult), reads=[B_ps[bk], B_cg[q]], writes=[B_in[q]])
                if tb + 1 < NTB:
                    S.op("act", lambda e: e.copy(out=inner[1 - q][:, 0:2], in_=inner[q][:, TB:TB + 2]),
                         reads=[B_in[q]], writes=[B_in[1 - q]])
            else:
                S.op("act", lambda e: e.activation(out=sg[q], in_=psum[bk][:, :], func=AF.Silu),
                     reads=[B_ps[bk]], writes=[B_sg[q]])
                S.op("dve", lambda e: e.tensor_scalar(out=tt[q], in0=inner[q][:, 0:TB], scalar1=cw[:, 3 * j:3 * j + 1],
                                                      scalar2=None, op0=ALU.mult),
                     reads=[B_in[q], B_cw], writes=[B_tt[q]])
                for tap in (1, 2):
                    S.op("dve", lambda e, tap=tap: e.scalar_tensor_tensor(
                        out=tt[q], in0=inner[q][:, tap:TB + tap], scalar=cw[:, 3 * j + tap:3 * j + tap + 1],
                        in1=tt[q], op0=ALU.mult, op1=ALU.add), reads=[B_in[q], B_cw, B_tt[q]], writes=[B_tt[q]])
                S.op("dve", lambda e: e.tensor_tensor(out=tt[q], in0=tt[q], in1=bg[q], op=ALU.mult),
                     reads=[B_tt[q], B_bg[q]], writes=[B_tt[q]])
                S.op("dve", lambda e: e.tensor_tensor(out=yb[q], in0=tt[q], in1=sg[q], op=ALU.mult),
                     reads=[B_tt[q], B_sg[q]], writes=[B_yb[q]])
                S.dma("sp", yT_part[j * 128:(j + 1) * 128, tb * TB:(tb + 1) * TB], yb[q],
                      reads=[B_yb[q]], writes=[B_yT_part])

        inproj(p["win"], colgroups, on_tile)
        S.barrier()

    def mixer_A1(p):
        AR.reset()
        gwv = min(512, EC)
        ncgv = EC // gwv
        sscols = AR.f32(ncgv * NTT)
        junk = AR.f32(512)
        vb = [AR.bf16(512) for _ in range(2)]
        B_ss, B_junk = Buf("sscols"), Buf("junk")
        B_vb = [Buf("vb0"), Buf("vb1")]
        S.op("dve", lambda e: e.memset(sscols, 0.0), writes=[B_ss])
        colgroups = [dict(c0=i * gwv, width=gwv, orient="TW") for i in range(ncgv)]
        cntv = [0]

        def on_tile_v(gi, tb, a, bk):
            q = cntv[0] % 2
            cntv[0] += 1
            col = gi * NTT + tb * 4 + a
            S.op("act", lambda e: e.activation(out=junk[:, 0:gwv], in_=psum[bk][:, 0:gwv], func=AF.Square),
                 reads=[B_ps[bk]], writes=[B_junk])
            S.op("dve", lambda e: e.reduce_sum(out=sscols[:, col:col + 1], in_=junk[:, 0:gwv], axis=mybir.AxisListType.X),
                 reads=[B_junk], writes=[B_ss])
            S.op("dve", lambda e: e.tensor_copy(out=vb[q][:, 0:gwv], in_=psum[bk][:, 0:gwv]),
                 reads=[B_ps[bk]], writes=[B_vb[q]])
            r0 = (tb * 4 + a) * 128
            S.dma("sp", vraw[r0:r0 + 128, gi * gwv:(gi + 1) * gwv], vb[q][:, 0:gwv], reads=[B_vb[q]], writes=[B_vraw])

        inproj(p["win"], colgroups, on_tile_v)
        ssv = AR.f32(NTT)
        B_ssv = Buf("ssv")
        S.op("dve", lambda e: e.tensor_copy(out=ssv, in_=sscols[:, 0:NTT]), reads=[B_ss], writes=[B_ssv])
        for i in range(1, ncgv):
            S.op("dve", lambda e, i=i: e.tensor_tensor(out=ssv, in0=ssv, in1=sscols[:, i * NTT:(i + 1) * NTT],
                                                        op=ALU.add), reads=[B_ss, B_ssv], writes=[B_ssv])
        S.dma("sp", ssv_part, ssv, reads=[B_ssv], writes=[B_ssv_part])
        S.barrier()

    def mixer_A2(p):
        AR.reset()
        rstdv = AR.f32(NTT)
        ssv8 = AR.f32(NCORES * NTT).rearrange("p (r t) -> p r t", r=NCORES)
        B_ssv8 = Buf("ssv8")
        vgain = AR.f32(HPC)
        wsTm = AR.f32(HPC * 128)
        biasr = AR.f32(HPC * 512)
        B_rv, B_vg, B_ws, B_bias = Buf("rstdv"), Buf("vgain"), Buf("wsTm"), Buf("biasr")
        S.dma("sp", ssv8, ssv_all.rearrange("(r p) t -> p r t", p=128), writes=[B_ssv8])
        S.op("dve", lambda e: e.tensor_copy(out=rstdv, in_=ssv8[:, 0, :]), reads=[B_ssv8], writes=[B_rv])
        for r in range(1, NCORES):
            S.op("dve", lambda e, r=r: e.tensor_tensor(out=rstdv, in0=rstdv, in1=ssv8[:, r, :], op=ALU.add),
                 reads=[B_ssv8, B_rv], writes=[B_rv])
        S.op("dve", lambda e: e.tensor_scalar(out=rstdv, in0=rstdv, scalar1=1.0 / E, scalar2=RMS_EPS,
                                              op0=ALU.mult, op1=ALU.add), reads=[B_rv], writes=[B_rv])
        S.op("act", lambda e: e.sqrt(out=rstdv, in_=rstdv), reads=[B_rv], writes=[B_rv])
        S.op("dve", lambda e: e.reciprocal(out=rstdv, in_=rstdv), reads=[B_rv], writes=[B_rv])
        S.dma("sp", vgain, p["vgain"], writes=[B_vg])
        S.dma("sp", wsTm, p["wsT"], writes=[B_ws])
        for g in range(HPC):
            S.op("dve", lambda e, g=g: e.tensor_tensor(out=wsTm[:, g * 128:(g + 1) * 128],
                                                        in0=wsTm[:, g * 128:(g + 1) * 128], in1=cmask_f, op=ALU.mult),
                 reads=[B_ws, B_const], writes=[B_ws])
        biasv = biasr.rearrange("p (g a t) -> p g a t", g=HPC, a=4)
        S.dma("sp", biasv[:, :, 0, :], p["wsb"].partition_broadcast(128).rearrange("p o (g t) -> p (o g) t", g=HPC),
              writes=[B_bias])
        for a in range(1, 4):
            S.op("dve", lambda e, a=a: e.tensor_copy(out=biasv[:, :, a, :], in_=biasv[:, :, 0, :]),
                 reads=[B_bias], writes=[B_bias])
        u = [[AR.f32(TB) for _ in range(2)] for _ in range(2)]
        sg = [[AR.f32(TB) for _ in range(2)] for _ in range(2)]
        vblk = [AR.bf16(4 * 256).rearrange("p (c x) -> p c x", c=4) for _ in range(2)]
        wss = [AR.bf16(128) for _ in range(4)]
        tmp = [AR.f32(TB) for _ in range(2)]
        yb = [AR.bf16(TB) for _ in range(2)]
        B_u = [[Buf("u"), Buf("u")] for _ in range(2)]
        B_sg = [[Buf("sg"), Buf("sg")] for _ in range(2)]
        B_vblk = [Buf("vblk0"), Buf("vblk1")]
        B_wss = [Buf(f"wss{i}") for i in range(4)]
        B_tmp = [Buf("tmp0"), Buf("tmp1")]
        B_yb = [Buf("yb0"), Buf("yb1")]
        c0 = EC
        colgroups = [dict(c0=c0 + pr * 512, width=512, orient="FFFF") for pr in range(HPC // 2)]
        wcnt = [0]
        ocnt = [0]

        def on_tile(pr, tb, cc, bk):
            q = tb % 2
            if cc == 0:
                src = vraw[tb * TB:(tb + 1) * TB, pr * 256:(pr + 1) * 256].rearrange("(c s) x -> s c x", s=128)
                S.dma("sp", vblk[q], src, reads=[B_vraw], writes=[B_vblk[q]])
            if cc < 2:
                S.op("act", lambda e: e.copy(out=u[q][cc], in_=psum[bk][:, :]), reads=[B_ps[bk]], writes=[B_u[q][cc]])
                return
            S.op("act", lambda e: e.activation(out=sg[q][cc - 2], in_=psum[bk][:, :], func=AF.Silu),
                 reads=[B_ps[bk]], writes=[B_sg[q][cc - 2]])
            if cc < 3:
                return
            for gi2 in range(2):
                g = 2 * pr + gi2
                b2 = 2 + (ocnt[0] % 2)
                o = ocnt[0] % 2
                ocnt[0] += 1
                for ch in range(4):
                    w = wcnt[0] % 4
                    wcnt[0] += 1
                    ti = tb * 4 + ch
                    S.op("dve", lambda e, w=w, ti=ti, g=g: e.tensor_scalar(
                        out=wss[w], in0=wsTm[:, g * 128:(g + 1) * 128], scalar1=rstdv[:, ti:ti + 1], scalar2=None,
                        op0=ALU.mult), reads=[B_ws, B_rv], writes=[B_wss[w]])
                    S.group("pe", [lambda e, w=w, ch=ch, gi2=gi2, b2=b2: e.matmul(
                        psum[b2][:, ch * 128:(ch + 1) * 128], lhsT=vblk[q][:, ch, gi2 * 128:(gi2 + 1) * 128],
                        rhs=wss[w], start=True, stop=True)], reads=[B_vblk[q], B_wss[w]], writes=[B_ps[b2]])
                S.op("dve", lambda e, g=g, b2=b2, o=o: e.scalar_tensor_tensor(
                    out=tmp[o], in0=psum[b2][:, :], scalar=vgain[:, g:g + 1], in1=biasr[:, g * 512:(g + 1) * 512],
                    op0=ALU.mult, op1=ALU.add), reads=[B_ps[b2], B_vg, B_bias], writes=[B_tmp[o]])
                S.op("dve", lambda e, o=o, gi2=gi2: e.tensor_tensor(out=tmp[o], in0=tmp[o], in1=u[q][gi2], op=ALU.mult),
                     reads=[B_tmp[o], B_u[q][gi2]], writes=[B_tmp[o]])
                S.op("dve", lambda e, o=o, gi2=gi2: e.tensor_tensor(out=yb[o], in0=tmp[o], in1=sg[q][gi2], op=ALU.mult),
                     reads=[B_tmp[o], B_sg[q][gi2]], writes=[B_yb[o]])
                S.dma("sp", yT_part[g * 128:(g + 1) * 128, tb * TB:(tb + 1) * TB], yb[o],
                      reads=[B_yb[o]], writes=[B_yT_part])

        inproj(p["win"], colgroups, on_tile)
        S.barrier()

    def mixer_B(p):
        AR.reset()
        NQT = NTB
        scale = 128.0 ** -0.5
        NH = NTT * HPC
        assert NH <= 512
        wf = AR.bf16(KD * HPC).rearrange("p (k h) -> p k h", k=KD)
        wf32 = AR.f32(KD * HPC).rearrange("p (k h) -> p k h", k=KD)
        B_wf32 = Buf("wf32")
        bft = AR.f32(HPC)
        cT = AR.f32(NH)
        cref = AR.f32(NH)
        mark = AR.off
        bfrep = AR.f32(NH)
        zt = AR.f32(NH)
        pp = [AR.f32(NH) for _ in range(2)]
        if AR.off - mark < 3 * (TB // 2) + 2 * TB + 2 * (TB // 2):
            AR.f32(3 * (TB // 2) + 2 * TB + 2 * (TB // 2) - (AR.off - mark))
        end1 = AR.off
        Bq = [AR.f32(4 * NTT).rearrange("p (q k) -> p q k", q=4) for _ in range(2)]
        B_Bq = [Buf("Bq0"), Buf("Bq1")]
        qT = AR.bf16(L)
        kT = AR.bf16(L)
        Vt = AR.bf16(L).rearrange("p (t d) -> p t d", d=128)
        sgT = AR.bf16(L)
        end2 = AR.off
        AR.off = mark
        pT = [AR.bf16(TB) for _ in range(3)]
        rs = AR.f32(TB)
        ot = AR.f32(TB)
        yb = [AR.bf16(TB) for _ in range(2)]
        assert AR.off <= end1
        AR.off = end2
        B_wf, B_bft, B_bfrep, B_zt, B_cT, B_cref, B_Bm = (Buf(n) for n in ("wf", "bft", "bfrep", "zt", "cT", "cref", "Bm_unused"))
        B_pp = [Buf("pp0"), Buf("pp1")]
        B_qT, B_kT, B_Vt, B_sgT = Buf("qT"), Buf("kT"), Buf("Vt"), Buf("sgT")
        B_pT = [Buf(f"pT{i}") for i in range(3)]
        B_rs, B_ot = Buf("rs"), Buf("ot")
        B_yb = [Buf("yb0"), Buf("yb1")]
        PSF = 7
        S.dma("sp", wf32, p["wf"].rearrange("(k p) h -> p k h", p=128), writes=[B_wf32])
        S.op("dve", lambda e: e.tensor_copy(out=wf, in_=wf32), reads=[B_wf32], writes=[B_wf])
        S.dma("sp", bft, p["bf"].partition_broadcast(128), writes=[B_bft])
        bfv = bfrep.rearrange("p (t h) -> p t h", h=HPC)
        S.op("dve", lambda e: e.tensor_copy(out=bfv[:, 0, :], in_=bft), reads=[B_bft], writes=[B_bfrep])
        d = 1
        while d < NTT:
            n = min(d, NTT - d)
            S.op("dve", lambda e, d=d, n=n: e.tensor_copy(out=bfv[:, d:d + n, :], in_=bfv[:, 0:n, :]),
                 reads=[B_bfrep], writes=[B_bfrep])
            d *= 2

        def extra_tb(tb, hTs, B_hTs):
            fns = []
            for a in range(4):
                ti = tb * 4 + a
                for k in range(KD):
                    fns.append(lambda e, k=k, a=a, ti=ti: e.matmul(
                        psum[PSF][:, ti * HPC:(ti + 1) * HPC], lhsT=hTs[:, k, a * 128:(a + 1) * 128], rhs=wf[:, k, :],
                        start=(k == 0), stop=(k == KD - 1)))
            S.group("pe", fns, reads=[B_hTs, B_wf], writes=[B_ps[PSF]])

        def forget_prep():
            S.op("dve", lambda e: e.tensor_tensor(out=zt, in0=psum[PSF][:, 0:NH], in1=bfrep, op=ALU.add),
                 reads=[B_ps[PSF], B_bfrep], writes=[B_zt])
            S.op("act", lambda e: e.activation(out=zt, in_=zt, func=AF.Exp, scale=-1.0), reads=[B_zt], writes=[B_zt])
            S.op("act", lambda e: e.activation(out=zt, in_=zt, func=AF.Ln, bias=1.0), reads=[B_zt], writes=[B_zt])
            S.op("dve", lambda e: e.tensor_scalar(out=zt, in0=zt, scalar1=-1.0, scalar2=None, op0=ALU.mult),
                 reads=[B_zt], writes=[B_zt])
            S.group("pe", [lambda e: e.matmul(psum[4][:, 0:NH], lhsT=cmask_f, rhs=zt, start=True, stop=True)],
                    reads=[B_zt, B_const], writes=[B_ps[4]])
            S.group("pe", [lambda e: e.matmul(psum[5][:, 0:NH], lhsT=ones_f, rhs=zt, start=True, stop=True)],
                    reads=[B_zt, B_const], writes=[B_ps[5]])
            S.op("dve", lambda e: e.tensor_copy(out=pp[0], in_=psum[5][:, 0:NH]), reads=[B_ps[5]], writes=[B_pp[0]])
            cur = 0
            d = 1
            while d < NTT:
                nxt = 1 - cur
                S.op("dve", lambda e, cur=cur, nxt=nxt, d=d: e.tensor_copy(out=pp[nxt][:, 0:d * HPC], in_=pp[cur][:, 0:d * HPC]),
                     reads=[B_pp[cur]], writes=[B_pp[nxt]])
                S.op("dve", lambda e, cur=cur, nxt=nxt, d=d: e.tensor_tensor(
                    out=pp[nxt][:, d * HPC:NH], in0=pp[cur][:, d * HPC:NH], in1=pp[cur][:, 0:NH - d * HPC], op=ALU.add),
                    reads=[B_pp[cur]], writes=[B_pp[nxt]])
                cur = nxt
                d *= 2
            S.op("dve", lambda e, cur=cur: e.tensor_copy(out=cref, in_=pp[cur]), reads=[B_pp[cur]], writes=[B_cref])
            S.op("dve", lambda e: e.tensor_tensor(out=cT, in0=cref, in1=psum[5][:, 0:NH], op=ALU.subtract),
                 reads=[B_cref, B_ps[5]], writes=[B_cT])
            S.op("dve", lambda e: e.tensor_tensor(out=cT, in0=cT, in1=psum[4][:, 0:NH], op=ALU.add),
                 reads=[B_cT, B_ps[4]], writes=[B_cT])
            S.barrier()

        colgroups = [dict(c0=h * 512, width=512, orient="FFTF") for h in range(HPC)]

        def on_tile(h, tb, cc, bk):
            sl = slice(tb * TB, (tb + 1) * TB)
            if cc == 0:
                S.op("act", lambda e: e.copy(out=qT[:, sl], in_=psum[bk][:, :]), reads=[B_ps[bk]], writes=[B_qT])
            elif cc == 1:
                S.op("dve", lambda e: e.tensor_copy(out=kT[:, sl], in_=psum[bk][:, :]), reads=[B_ps[bk]], writes=[B_kT])
            elif cc == 2:
                S.op("dve", lambda e: e.tensor_copy(out=Vt[:, tb * 4:tb * 4 + 4, :],
                                                    in_=psum[bk][:, :].rearrange("p (a d) -> p a d", d=128)),
                     reads=[B_ps[bk]], writes=[B_Vt])
            else:
                S.op("act", lambda e: e.activation(out=sgT[:, sl], in_=psum[bk][:, :], func=AF.Silu),
                     reads=[B_ps[bk]], writes=[B_sgT])

        crefv = cref.rearrange("p (t h) -> p t h", h=HPC)
        cTv = cT.rearrange("p (t h) -> p t h", h=HPC)

        def attention(h):
            if h == 0:
                forget_prep()
            steps = []
            for qt in range(NQT):
                for kt in range(4 * qt + 4):
                    steps.append((qt, kt))

            def issue_S(idx):
                qt, kt = steps[idx]
                i = kt - 4 * qt
                q0 = max(i, 0)
                bk = 2 + idx % 2
                S.group("pe", [lambda e: e.matmul(psum[bk][:, q0 * 128:TB], lhsT=kT[:, kt * 128:(kt + 1) * 128],
                                                   rhs=qT[:, qt * TB + q0 * 128:(qt + 1) * TB], start=True, stop=True)],
                        reads=[B_kT, B_qT], writes=[B_ps[bk]])

            def step(idx, qt, kt):
                i = kt - 4 * qt
                q0 = max(i, 0)
                bk = 2 + idx % 2
                sl = idx % 3
                par = qt % 2
                ob, sb = 4 + par, 6 + par
                nk = 4 * qt + 4
                if kt == 0:
                    for qs in range(4):
                        S.op("dve", lambda e, qs=qs: e.tensor_scalar(
                            out=Bq[par][:, qs, 0:nk], in0=cTv[:, 0:nk, h],
                            scalar1=crefv[:, 4 * qt + qs, h:h + 1], scalar2=-1.0, op0=ALU.subtract, op1=ALU.mult),
                            reads=[B_cref, B_cT], writes=[B_Bq[par]])
                for qs in range(q0, 4):
                    S.op("act", lambda e, qs=qs: e.activation(
                        out=pT[sl][:, qs * 128:(qs + 1) * 128], in_=psum[bk][:, qs * 128:(qs + 1) * 128], func=AF.Exp,
                        bias=Bq[par][:, qs, kt:kt + 1], scale=scale),
                        reads=[B_ps[bk], B_Bq[par]], writes=[B_pT[sl]])
                if i >= 0:
                    S.op("dve", lambda e: e.tensor_tensor(out=pT[sl][:, i * 128:(i + 1) * 128],
                                                          in0=pT[sl][:, i * 128:(i + 1) * 128], in1=cmask_b, op=ALU.mult),
                         reads=[B_pT[sl], B_const], writes=[B_pT[sl]])
                S.group("pe", [
                    lambda e: e.matmul(psum[ob][:, q0 * 128:TB], lhsT=Vt[:, kt, :], rhs=pT[sl][:, q0 * 128:TB],
                                       start=(kt == 0), stop=(kt == nk - 1), skip_group_check=True),
                    lambda e: e.matmul(psum[sb][:, q0 * 128:TB], lhsT=ones_b, rhs=pT[sl][:, q0 * 128:TB],
                                       start=(kt == 0), stop=(kt == nk - 1), skip_group_check=True)],
                    reads=[B_Vt, B_pT[sl], B_const], writes=[B_ps[ob], B_ps[sb]])
                if kt == nk - 1:
                    o = qt % 2
                    S.op("dve", lambda e: e.reciprocal(out=rs, in_=psum[sb][:, :]), reads=[B_ps[sb]], writes=[B_rs])
                    S.op("dve", lambda e: e.tensor_tensor(out=ot, in0=psum[ob][:, :], in1=rs, op=ALU.mult),
                         reads=[B_ps[ob], B_rs], writes=[B_ot])
                    S.op("dve", lambda e: e.tensor_tensor(out=yb[o], in0=ot, in1=sgT[:, qt * TB:(qt + 1) * TB], op=ALU.mult),
                         reads=[B_ot, B_sgT], writes=[B_yb[o]])
                    S.dma("sp", yT_part[h * 128:(h + 1) * 128, qt * TB:(qt + 1) * TB], yb[o],
                          reads=[B_yb[o]], writes=[B_yT_part])

            issue_S(0)
            for idx, (qt, kt) in enumerate(steps):
                if idx + 1 < len(steps):
                    issue_S(idx + 1)
                step(idx, qt, kt)

        inproj(p["win"], colgroups, on_tile, post_group=attention, extra_tb=extra_tb)
        S.barrier()

    def outproj(p):
        AR.reset()
        Wo = AR.bf16(KE * FCR).rearrange("p (k c) -> p k c", k=KE)
        NP = 4 if KE % 4 == 0 else 1
        KP = KE // NP
        NSL = 4
        ysl = [AR.bf16(KP * TB).rearrange("p (k t) -> p k t", k=KP) for _ in range(NSL)]
        xb = [AR.f32(FC * TB).rearrange("p (j t) -> p j t", j=FC) for _ in range(2)]
        xn = [AR.f32(FC * TB).rearrange("p (j t) -> p j t", j=FC) for _ in range(2)]
        KPS = min(4, KE)
        stg = [AR.f32(KPS * FCR).rearrange("p (k c) -> p k c", k=KPS) for _ in range(2)]
        B_stg = [Buf("ostg0"), Buf("ostg1")]
        B_Wo = Buf("Wo")
        B_ysl = [Buf(f"ysl{i}") for i in range(NSL)]
        B_xb = [Buf("xb0"), Buf("xb1")]
        B_xn = [Buf("xn0"), Buf("xn1")]
        for q in range(KE // KPS):
            sp_ = q % 2
            S.dma("sp", stg[sp_], p["wout"][q * KPS * 128:(q + 1) * KPS * 128, :].rearrange("(k p) c -> p k c", p=128),
                  writes=[B_stg[sp_]])
            S.op("dve", lambda e, q=q, sp_=sp_: e.tensor_copy(out=Wo[:, q * KPS:(q + 1) * KPS, :], in_=stg[sp_]),
                 reads=[B_stg[sp_]], writes=[B_Wo])
        scnt = 0
        for tb in range(NTB):
            par = tb % 2
            S.dma("sp", xb[par], xblk(xres, tb), reads=[B_xres[tb]], writes=[B_xb[par]])
            for pc in range(NP):
                sl = scnt % NSL
                scnt += 1
                k0, k1 = pc * KP, (pc + 1) * KP
                S.dma("sp", ysl[sl],
                      yT_full[k0 * 128:k1 * 128, tb * TB:(tb + 1) * TB].rearrange("(k p) t -> p k t", p=128),
                      reads=[B_yT_full], writes=[B_ysl[sl]])
                for cc in range(FC):
                    bk = par * 4 + cc
                    fns = [lambda e, k=k, cc=cc, bk=bk, sl=sl, pc=pc: e.matmul(
                        psum[bk][:, :], lhsT=Wo[:, pc * KP + k, cc * 128:(cc + 1) * 128], rhs=ysl[sl][:, k, :],
                        start=(pc == 0 and k == 0), stop=(pc == NP - 1 and k == KP - 1), skip_group_check=True)
                        for k in range(KP)]
                    S.group("pe", fns, reads=[B_Wo, B_ysl[sl]], writes=[B_ps[bk]])
            for cc in range(FC):
                bk = par * 4 + cc
                S.op("dve", lambda e, cc=cc, bk=bk, par=par: e.tensor_tensor(
                    out=xn[par][:, cc, :], in0=psum[bk][:, :], in1=xb[par][:, cc, :], op=ALU.add),
                    reads=[B_ps[bk], B_xb[par]], writes=[B_xn[par]])
            S.dma("sp", xblk(xout, tb), xn[par], reads=[B_xn[par]], writes=[B_xout[tb]])
        S.barrier()

    if kind == "ss0":
        sumsq_pass(xres, B_xres)
    elif kind == "norm":
        norm_stage(p["norm"], to_output=False)
    elif kind == "final":
        norm_stage(p["norm"], to_output=True)
    elif kind == "mixA1":
        mixer_A1(p)
    elif kind == "mixA2":
        mixer_A2(p)
    elif kind == "mixB":
        mixer_B(p)
    elif kind == "mixC":
        mixer_C(p)
    elif kind == "out":
        outproj(p)
        sumsq_pass(xout, B_xout)
    S.barrier()

    with nc.Block() as block:
        S.emit(block)
    stack.close()
    return nc


def _pp(vec, n):
    return np.ascontiguousarray(np.asarray(vec, np.float32).reshape(n, 128).T)


def layer_inputs(cfg, li, c, inp):
    D, E, EC, HPC, FC, FCR = cfg.D, cfg.E, cfg.EC, cfg.HPC, cfg.FC, cfg.FCR
    m = MIX[li]
    pre = f"l{li}_"
    d = {}
    d["gain"] = _pp(inp[pre + "norm"][c * FCR:(c + 1) * FCR], FC)
    w_in = inp[pre + "w_in"]
    ch = slice(c * EC, (c + 1) * EC)
    if m == "A":
        u = w_in[:, 0 * E:1 * E][:, ch]
        v = w_in[:, 1 * E:2 * E][:, ch]
        g = w_in[:, 2 * E:3 * E][:, ch]
        cols = [v]
        for pr in range(HPC // 2):
            a, b = 2 * pr, 2 * pr + 1
            cols += [u[:, a * 128:(a + 1) * 128], u[:, b * 128:(b + 1) * 128],
                     g[:, a * 128:(a + 1) * 128], g[:, b * 128:(b + 1) * 128]]
        d["win"] = np.ascontiguousarray(np.concatenate(cols, axis=1))
        d["vgain"] = _pp(inp[pre + "v_gain"][ch], HPC)
        ws = inp[pre + "ws"][c * HPC:(c + 1) * HPC]
        d["wsT"] = np.ascontiguousarray(np.transpose(ws, (2, 0, 1)).reshape(128, HPC * 128))
        d["wsb"] = np.ascontiguousarray(inp[pre + "ws_bias"][c * HPC:(c + 1) * HPC].reshape(1, HPC * 128))
    else:
        parts = [w_in[:, j * E:(j + 1) * E][:, ch] for j in range(4)]
        cols = []
        for h in range(HPC):
            for j in range(4):
                cols.append(parts[j][:, h * 128:(h + 1) * 128])
        d["win"] = np.ascontiguousarray(np.concatenate(cols, axis=1))
        if m == "B":
            d["wf"] = np.ascontiguousarray(w_in[:, 4 * E + c * HPC:4 * E + (c + 1) * HPC])
            d["bf"] = np.ascontiguousarray(inp[pre + "b_f"][c * HPC:(c + 1) * HPC].reshape(1, HPC))
        else:
            cw = inp[pre + "conv_w"][:, ch]
            d["convw"] = np.ascontiguousarray(
                np.transpose(cw.reshape(3, HPC, 128), (2, 1, 0)).reshape(128, HPC * 3))
    d["wout"] = np.ascontiguousarray(inp[pre + "w_out"][:, c * FCR:(c + 1) * FCR])
    return d


_CONST = None


def consts():
    global _CONST
    if _CONST is None:
        _CONST = {"cmask": np.triu(np.ones((128, 128), np.float32)), "ident": np.eye(128, dtype=np.float32)}
    return _CONST


_PROGS = {}


def launch(cfg, kind, per_core):
    key = (cfg.L, cfg.D, cfg.E, kind)
    if key not in _PROGS:
        _PROGS[key] = build_segment(cfg, kind)
    nc = _PROGS[key]
    in_maps = []
    for c in range(NCORES):
        m = dict(per_core[c])
        m.update(consts())
        in_maps.append(m)
    res = run_bass_kernel_spmd(nc, in_maps, core_ids=list(range(NCORES)))
    return res.results


def run_forward(cfg, inp, layers=(0, 1, 2, 3)):
    inp = {k: np.asarray(v) for k, v in inp.items()}
    x = inp["x"][0]
    FCR = cfg.FCR
    C = range(NCORES)
    xT = [np.ascontiguousarray(x[:, c * FCR:(c + 1) * FCR].T) for c in C]
    r = launch(cfg, "ss0", [{"xT": xT[c]} for c in C])
    ss_all = np.ascontiguousarray(np.concatenate([r[c]["ss_part"] for c in C], axis=0))
    for li in layers:
        lp = [layer_inputs(cfg, li, c, inp) for c in C]
        r = launch(cfg, "norm", [{"xT": xT[c], "ss_all": ss_all, "gain": lp[c]["gain"]} for c in C])
        hT_full = np.ascontiguousarray(np.concatenate([r[c]["hT_part"] for c in C], axis=0))
        m = MIX[li]
        if m == "A":
            r = launch(cfg, "mixA1", [{"hT_full": hT_full, "win": lp[c]["win"]} for c in C])
            ssv_all = np.ascontiguousarray(np.concatenate([r[c]["ssv_part"] for c in C], axis=0))
            vraw = [r[c]["vraw"] for c in C]
            r = launch(cfg, "mixA2", [{"hT_full": hT_full, "win": lp[c]["win"], "ssv_all": ssv_all, "vraw": vraw[c],
                                       "vgain": lp[c]["vgain"], "wsT": lp[c]["wsT"], "wsb": lp[c]["wsb"]} for c in C])
            del vraw
        elif m == "B":
            r = launch(cfg, "mixB", [{"hT_full": hT_full, "win": lp[c]["win"], "wf": lp[c]["wf"], "bf": lp[c]["bf"]}
                                     for c in C])
        else:
            r = launch(cfg, "mixC", [{"hT_full": hT_full, "win": lp[c]["win"], "convw": lp[c]["convw"]} for c in C])
        del hT_full
        yT_full = np.ascontiguousarray(np.concatenate([r[c]["yT_part"] for c in C], axis=0))
        r = launch(cfg, "out", [{"yT_full": yT_full, "xT": xT[c], "wout": lp[c]["wout"]} for c in C])
        del yT_full, lp
        xT = [np.ascontiguousarray(r[c]["xT_out"]) for c in C]
        ss_all = np.ascontiguousarray(np.concatenate([r[c]["ss_part"] for c in C], axis=0))
    r = launch(cfg, "final", [{"xT": xT[c], "ss_all": ss_all,
                               "gain": _pp(inp["final_norm"][c * FCR:(c + 1) * FCR], cfg.FC)} for c in C])
    out = np.concatenate([r[c]["out"] for c in C], axis=1)
    return out[None].astype(np.float32)


def kernel(**inputs):
    return run_forward(Cfg(), inputs)
```

```python
import numpy as np
import ml_dtypes
from contextlib import ExitStack
import concourse.bass as bass
import concourse.mybir as mybir
from concourse.bass_utils import run_bass_kernel_spmd

F32 = mybir.dt.float32
BF16 = mybir.dt.bfloat16
AF = mybir.ActivationFunctionType
ALU = mybir.AluOpType
NCORES = 8
RMS_EPS = 1e-6
TB = 512


class Cfg:
    def __init__(self, L=8192, D=4096, E=8192):
        self.L, self.D, self.E = L, D, E
        self.FC = D // NCORES // 128
        self.FCR = self.FC * 128
        self.HPC = E // NCORES // 128
        self.EC = self.HPC * 128
        self.KD = D // 128
        self.KE = E // 128
        self.NTB = L // TB
        self.NTT = L // 128


MIX = ("A", "B", "C", "A")


class Buf:
    __slots__ = ("name", "w", "r", "dsem")

    def __init__(self, name):
        self.name = name
        self.w = None
        self.r = {}
        self.dsem = None


class Sched:
    ENGS = ("pe", "act", "dve", "pool", "sp")

    def __init__(self, nc, stack):
        self.nc = nc
        self.stack = stack
        self.lists = {e: [] for e in self.ENGS}
        self.semh = {}
        self.semcnt = {}
        self.seen = {e: {} for e in self.ENGS}
        self.esem = {}
        for e in ("pe", "act", "dve", "pool"):
            self.esem[e] = self.new_sem("e_" + e)
        self.nsem = 0

    def new_sem(self, name):
        key = name + "_%d" % len(self.semh)
        self.semh[key] = self.stack.enter_context(self.nc.semaphore(key))
        self.semcnt[key] = 0
        return key

    def _waits(self, eng, toks):
        seen = self.seen[eng]
        own = self.esem.get(eng)
        for (s, v) in toks:
            if eng == "pe" and s == own:
                continue
            if seen.get(s, 0) >= v:
                continue
            seen[s] = v
            h = self.semh[s]
            self.lists[eng].append(lambda e, h=h, v=v: e.wait_ge(h, v))

    @staticmethod
    def _deps(reads, writes, deps):
        toks = list(deps)
        for b in reads:
            if b.w is not None:
                toks.append(b.w)
        for b in writes:
            if b.w is not None:
                toks.append(b.w)
            toks.extend(b.r.items())
        return toks

    @staticmethod
    def _mark(tok, reads, writes):
        s, v = tok
        for b in reads:
            if b.r.get(s, 0) < v:
                b.r[s] = v
        for b in writes:
            b.w = tok
            b.r = {}

    def op(self, eng, fn, reads=(), writes=(), deps=()):
        self._waits(eng, self._deps(reads, writes, deps))
        s = self.esem[eng]
        self.semcnt[s] += 1
        tok = (s, self.semcnt[s])
        h = self.semh[s]
        self.lists[eng].append(lambda e, fn=fn, h=h: fn(e).then_inc(h, 1))
        self._mark(tok, reads, writes)
        return tok

    def group(self, eng, fns, reads=(), writes=(), deps=()):
        self._waits(eng, self._deps(reads, writes, deps))
        s = self.esem[eng]
        self.semcnt[s] += 1
        tok = (s, self.semcnt[s])
        h = self.semh[s]
        lst = self.lists[eng]
        for fn in fns[:-1]:
            lst.append(fn)
        lst.append(lambda e, fn=fns[-1], h=h: fn(e).then_inc(h, 1))
        self._mark(tok, reads, writes)
        return tok

    def dma(self, q, out_ap, in_ap, reads=(), writes=(), deps=(), sembuf=None):
        sb = sembuf if sembuf is not None else writes[0]
        if sb.dsem is None:
            sb.dsem = self.new_sem("d_" + sb.name)
        s = sb.dsem
        self._waits(q, self._deps(reads, writes, deps))
        self.semcnt[s] += 16
        tok = (s, self.semcnt[s])
        h = self.semh[s]
        self.lists[q].append(lambda e, o=out_ap, i=in_ap, h=h: e.dma_start(out=o, in_=i).then_inc(h, 16))
        self._mark(tok, reads, writes)
        return tok

    def collective(self, kind, op, in_ap, out_ap, reads=(), writes=()):
        import os
        if os.environ.get("NOCC"):
            n = in_ap.shape[0]
            return self.dma("sp", out_ap[0:n, :], in_ap, reads=reads, writes=writes)
        s = self.new_sem("cc")
        self._waits("pool", self._deps(reads, writes, ()))
        self.semcnt[s] = 1
        tok = (s, 1)
        h = self.semh[s]
        self.lists["pool"].append(
            lambda e, h=h: e.collective_compute(kind, op, replica_groups=[list(range(NCORES))],
                                                ins=[in_ap], outs=[out_ap]).then_inc(h, 1))
        self._mark(tok, reads, writes)
        self._waits("pool", [tok])
        return tok

    def barrier(self):
        toks = [(s, v) for s, v in self.semcnt.items() if v > 0]
        for e in self.ENGS:
            self._waits(e, toks)

    def emit(self, block):
        nc = self.nc
        L = self.lists

        @block.tensor
        def _(e):
            for f in L["pe"]:
                f(e)

        @block.scalar
        def _(e):
            for f in L["act"]:
                f(e)

        @block.vector
        def _(e):
            for f in L["dve"]:
                f(e)

        @block.gpsimd
        def _(e):
            for f in L["pool"]:
                f(e)

        @block.sync
        def _(e):
            for f in L["sp"]:
                f(e)


class Arena:
    def __init__(self, t, words):
        self.t = t
        self.words = words
        self.off = 0

    def reset(self):
        self.off = 0

    def f32(self, n):
        assert self.off + n <= self.words, ("SBUF arena overflow", self.off, n, self.words)
        ap = self.t[:, self.off:self.off + n]
        self.off += n
        return ap

    def bf16(self, n):
        w = (n + 1) // 2
        return self.f32(w).bitcast(BF16)


def build_segment(cfg, kind):
    L, D, E = cfg.L, cfg.D, cfg.E
    FC, FCR, HPC, EC, KD, KE, NTB, NTT = cfg.FC, cfg.FCR, cfg.HPC, cfg.EC, cfg.KD, cfg.KE, cfg.NTB, cfg.NTT
    nc = bass.Bass("TRN2", target_bir_lowering=False)
    stack = ExitStack()
    S = Sched(nc, stack)

    def ext_in(name, shape, dt=F32):
        return nc.dram_tensor(name, list(shape), dt, kind="ExternalInput").ap()

    def ext_out(name, shape, dt=F32):
        return nc.dram_tensor(name, list(shape), dt, kind="ExternalOutput").ap()

    cmask_in = ext_in("cmask", [128, 128])
    ident_in = ext_in("ident", [128, 128])
    p = {}
    xres = xout = hT_part = hT_full = yT_part = yT_full = ss_part = ss_all = ssv_part = ssv_all = vraw = out_ext = None
    if kind == "ss0":
        xres = ext_in("xT", [FCR, L])
        ss_part = ext_out("ss_part", [1, L])
    elif kind in ("norm", "final"):
        xres = ext_in("xT", [FCR, L])
        ss_all = ext_in("ss_all", [NCORES, L])
        p["norm"] = ext_in("gain", [128, FC])
        if kind == "norm":
            hT_part = ext_out("hT_part", [FCR, L], BF16)
        else:
            out_ext = ext_out("out", [L, FCR])
    elif kind.startswith("mix"):
        hT_full = ext_in("hT_blk", [NTB * 128, KD * TB], BF16)
        m = kind[3]
        p["win"] = ext_in("win", [128, KD * (3 * EC if m == "A" else 4 * EC)])
        if kind == "mixA1":
            ssv_part = ext_out("ssv_part", [128, NTT])
            vraw = ext_out("vraw", [L, EC], BF16)
        else:
            yT_part = ext_out("yT_part", [EC, L], BF16)
        if kind == "mixA2":
            ssv_all = ext_in("ssv_all", [NCORES * 128, NTT])
            vraw = ext_in("vraw", [L, EC], BF16)
            p["vgain"] = ext_in("vgain", [128, HPC])
            p["wsT"] = ext_in("wsT", [128, HPC * 128])
            p["wsb"] = ext_in("wsb", [1, HPC * 128])
        elif kind == "mixB":
            p["wf"] = ext_in("wf", [D, HPC])
            p["bf"] = ext_in("bf", [1, HPC])
        elif kind == "mixC":
            p["convw"] = ext_in("convw", [128, HPC * 3])
    elif kind == "out":
        yT_full = ext_in("yT_blk", [NTB * 4 * 128, (KE // 4) * TB], BF16)
        xres = ext_in("xT", [FCR, L])
        p["wout"] = ext_in("wout", [128, KE * FCR])
        xout = ext_out("xT_out", [FCR, L])
        ss_part = ext_out("ss_part", [1, L])
    else:
        raise ValueError(kind)

    B_xres = [Buf(f"xres{tb}") for tb in range(NTB)]
    B_xout = [Buf(f"xout{tb}") for tb in range(NTB)]
    B_hT_part, B_hT_full = Buf("hT_part"), Buf("hT_full")
    B_yT_part, B_yT_full = Buf("yT_part"), Buf("yT_full")
    B_ss_part, B_ss_full = Buf("ss_part"), Buf("ss_full")
    B_ssv_part, B_ssv_full = Buf("ssv_part"), Buf("ssv_full")
    B_vraw = Buf("vraw")

    ARENA_WORDS = 51200
    arena_t = stack.enter_context(nc.sbuf_tensor("arena", [128, ARENA_WORDS], F32))
    const_t = stack.enter_context(nc.sbuf_tensor("consts", [128, 1024], F32))
    AR = Arena(arena_t, ARENA_WORDS)
    psum = [stack.enter_context(nc.psum_tensor(f"ps{i}", [128, 512], F32)) for i in range(8)]
    B_ps = [Buf(f"ps{i}") for i in range(8)]

    cmask_f = const_t[:, 0:128]
    ident_f = const_t[:, 128:256]
    ones_f = const_t[:, 256:384]
    cmask_b = const_t[:, 384:448].bitcast(BF16)
    ones_b = const_t[:, 448:512].bitcast(BF16)
    B_const = Buf("const")
    S.dma("sp", cmask_f, cmask_in, writes=[B_const])
    S.dma("sp", ident_f, ident_in, writes=[B_const])
    S.op("dve", lambda e: e.memset(ones_f, 1.0), writes=[B_const])
    S.op("dve", lambda e: e.memset(ones_b, 1.0), writes=[B_const])
    S.op("dve", lambda e: e.tensor_copy(out=cmask_b, in_=cmask_f), reads=[B_const], writes=[B_const])
    S.barrier()

    def xblk(src, tb):
        return src[:, tb * TB:(tb + 1) * TB].rearrange("(j p) t -> p j t", p=128)

    def sumsq_pass(src, B_src):
        AR.reset()
        xb = [AR.f32(FC * TB).rearrange("p (j t) -> p j t", j=FC) for _ in range(2)]
        sq = [AR.f32(FC * TB).rearrange("p (j t) -> p j t", j=FC) for _ in range(2)]
        ssb = [AR.f32(TB) for _ in range(2)]
        B_xb = [Buf("sxb0"), Buf("sxb1")]
        B_sq = [Buf("sq0"), Buf("sq1")]
        B_ssb = [Buf("ssb0"), Buf("ssb1")]
        for tb in range(NTB):
            p_ = tb % 2
            S.dma("sp", xb[p_], xblk(src, tb), reads=[B_src[tb]], writes=[B_xb[p_]])
            S.op("act", lambda e, p_=p_: e.activation(out=sq[p_], in_=xb[p_], func=AF.Square),
                 reads=[B_xb[p_]], writes=[B_sq[p_]])
            fns = []
            for j in range(FC):
                fns.append(lambda e, p_=p_, j=j: e.matmul(psum[p_][:, :], lhsT=ones_f, rhs=sq[p_][:, j, :],
                                                           start=(j == 0), stop=(j == FC - 1)))
            S.group("pe", fns, reads=[B_sq[p_], B_const], writes=[B_ps[p_]])
            S.op("dve", lambda e, p_=p_: e.tensor_copy(out=ssb[p_][0:1, :], in_=psum[p_][0:1, :]),
                 reads=[B_ps[p_]], writes=[B_ssb[p_]])
            S.dma("pool", ss_part[:, tb * TB:(tb + 1) * TB], ssb[p_][0:1, :], reads=[B_ssb[p_]], writes=[B_ss_part])
        S.barrier()

    def norm_stage(gain_in, to_output):
        AR.reset()
        rstd_t = AR.f32(L)
        rtmp = AR.f32(L)
        gain_t = AR.f32(FC)
        xb = [AR.f32(FC * TB).rearrange("p (j t) -> p j t", j=FC) for _ in range(2)]
        B_rstd, B_gain, B_rtmp = Buf("rstd"), Buf("gain"), Buf("rtmp")
        B_xb = [Buf("xb0"), Buf("xb1")]
        S.dma("sp", gain_t, gain_in, writes=[B_gain])
        S.dma("sp", rstd_t, ss_all[0:1, :].partition_broadcast(128), reads=[B_ss_full], writes=[B_rstd])
        for r in range(1, NCORES):
            S.dma("sp", rtmp, ss_all[r:r + 1, :].partition_broadcast(128), reads=[B_ss_full], writes=[B_rtmp])
            S.op("dve", lambda e: e.tensor_tensor(out=rstd_t, in0=rstd_t, in1=rtmp, op=ALU.add),
                 reads=[B_rstd, B_rtmp], writes=[B_rstd])
        S.op("dve", lambda e: e.tensor_scalar(out=rstd_t, in0=rstd_t, scalar1=1.0 / D, scalar2=RMS_EPS,
                                              op0=ALU.mult, op1=ALU.add), reads=[B_rstd], writes=[B_rstd])
        S.op("act", lambda e: e.sqrt(out=rstd_t, in_=rstd_t), reads=[B_rstd], writes=[B_rstd])
        S.op("dve", lambda e: e.reciprocal(out=rstd_t, in_=rstd_t), reads=[B_rstd], writes=[B_rstd])
        if not to_output:
            hb = [AR.bf16(FC * TB).rearrange("p (j t) -> p j t", j=FC) for _ in range(2)]
            B_hb = [Buf("hb0"), Buf("hb1")]
            for tb in range(NTB):
                p = tb % 2
                S.dma("sp", xb[p], xblk(xres, tb), reads=[B_xres[tb]], writes=[B_xb[p]])
                for j in range(FC):
                    S.op("dve", lambda e, p=p, j=j, tb=tb: e.scalar_tensor_tensor(
                        out=hb[p][:, j, :], in0=xb[p][:, j, :], scalar=gain_t[:, j:j + 1],
                        in1=rstd_t[:, tb * TB:(tb + 1) * TB], op0=ALU.mult, op1=ALU.mult),
                        reads=[B_xb[p], B_rstd, B_gain], writes=[B_hb[p]])
                S.dma("pool", xblk(hT_part, tb), hb[p], reads=[B_hb[p]], writes=[B_hT_part])
        else:
            hf = [AR.f32(FC * TB).rearrange("p (j t) -> p j t", j=FC) for _ in range(2)]
            ob = [AR.f32(4 * FCR).rearrange("p (a c) -> p a c", a=4) for _ in range(2)]
            B_hf = [Buf("hf0"), Buf("hf1")]
            B_ob = [Buf("ob0"), Buf("ob1")]
            B_out = Buf("out")
            for tb in range(NTB):
                p = tb % 2
                S.dma("sp", xb[p], xblk(xres, tb), reads=[B_xres[tb]], writes=[B_xb[p]])
                for j in range(FC):
                    S.op("dve", lambda e, p=p, j=j, tb=tb: e.scalar_tensor_tensor(
                        out=hf[p][:, j, :], in0=xb[p][:, j, :], scalar=gain_t[:, j:j + 1],
                        in1=rstd_t[:, tb * TB:(tb + 1) * TB], op0=ALU.mult, op1=ALU.mult),
                        reads=[B_xb[p], B_rstd, B_gain], writes=[B_hf[p]])
                for a in range(4):
                    bk = (tb * 4 + a) % 4
                    fns = []
                    for j in range(FC):
                        fns.append(lambda e, p=p, a=a, j=j, bk=bk: e.transpose(
                            psum[bk][:, j * 128:(j + 1) * 128], hf[p][:, j, a * 128:(a + 1) * 128], ident_f))
                    S.group("pe", fns, reads=[B_hf[p], B_const], writes=[B_ps[bk]])
                    S.op("act", lambda e, p=p, a=a, bk=bk: e.copy(out=ob[p][:, a, :], in_=psum[bk][:, 0:FCR]),
                         reads=[B_ps[bk]], writes=[B_ob[p]])
                S.dma("pool", out_ext[tb * TB:(tb + 1) * TB, :].rearrange("(a p) c -> p a c", p=128), ob[p],
                      reads=[B_ob[p]], writes=[B_out])
        S.barrier()

    def inproj(win, colgroups, on_tile, pre_group=None, post_group=None, extra_tb=None, ps_banks=(0, 1), nwbuf=1):
        hT = [AR.bf16(KD * TB).rearrange("p (k t) -> p k t", k=KD) for _ in range(2)]
        B_hT = [Buf("hT0"), Buf("hT1")]
        gwmax = max(g["width"] for g in colgroups)
        Wts = [AR.bf16(KD * gwmax) for _ in range(nwbuf)]
        KPS = min(4, KD)
        NPIECE = KD // KPS
        stg = [AR.f32(KPS * gwmax) for _ in range(2)]
        B_stg = [Buf("stg0"), Buf("stg1")]
        B_Ws = [[Buf(f"W{b}_{q}") for q in range(NPIECE)] for b in range(nwbuf)]
        cnt = 0
        scnt = [0]

        def load_W(gi):
            g = colgroups[gi]
            gw = g["width"]
            W = Wts[gi % nwbuf][:, 0:KD * gw].rearrange("p (k c) -> p k c", k=KD)
            for q in range(NPIECE):
                sp_ = scnt[0] % 2
                scnt[0] += 1
                st = stg[sp_][:, 0:KPS * gw].rearrange("p (k c) -> p k c", k=KPS)
                o0 = KD * g["c0"] + q * KPS * gw
                src = win[:, o0:o0 + KPS * gw].rearrange("p (k c) -> p k c", k=KPS)
                S.dma("sp", st, src, writes=[B_stg[sp_]])
                S.op("dve", lambda e, st=st, W=W, q=q: e.tensor_copy(out=W[:, q * KPS:(q + 1) * KPS, :], in_=st),
                     reads=[B_stg[sp_]], writes=[B_Ws[gi % nwbuf][q]])

        load_W(0)
        for gi, g in enumerate(colgroups):
            gw = g["width"]
            W = Wts[gi % nwbuf][:, 0:KD * gw].rearrange("p (k c) -> p k c", k=KD)
            B_W = B_Ws[gi % nwbuf]
            if nwbuf == 1 and gi > 0:
                load_W(gi)
            if pre_group is not None:
                pre_group(gi)
            for tb in range(NTB):
                if nwbuf == 2 and tb == NTB // 2 and gi + 1 < len(colgroups):
                    load_W(gi + 1)
                hp = (gi * NTB + tb) % 2
                HQ = 4 if KD % 4 == 0 else 1
                KQ = KD // HQ
                for q in range(HQ):
                    src = hT_full[tb * 128:(tb + 1) * 128, q * KQ * TB:(q + 1) * KQ * TB].rearrange(
                        "p (k t) -> p k t", k=KQ)
                    S.dma("sp", hT[hp][:, q * KQ:(q + 1) * KQ, :], src, reads=[B_hT_full], writes=[B_hT[hp]])
                if extra_tb is not None and gi == 0:
                    extra_tb(tb, hT[hp], B_hT[hp])
                if g["orient"] == "TW":
                    for a in range(4):
                        bk = ps_banks[cnt % 2]
                        cnt += 1
                        fns = [lambda e, k=k, a=a, bk=bk, hp=hp, W=W, gw=gw: e.matmul(
                            psum[bk][:, 0:gw], lhsT=hT[hp][:, k, a * 128:(a + 1) * 128], rhs=W[:, k, :],
                            start=(k == 0), stop=(k == KD - 1)) for k in range(KD)]
                        S.group("pe", fns, reads=[B_hT[hp]] + B_W, writes=[B_ps[bk]])
                        on_tile(gi, tb, a, bk)
                else:
                    for cc, orient in enumerate(g["orient"]):
                        bk = ps_banks[cnt % 2]
                        cnt += 1
                        if orient == "F":
                            fns = [lambda e, k=k, cc=cc, bk=bk, hp=hp, W=W: e.matmul(
                                psum[bk][:, :], lhsT=W[:, k, cc * 128:(cc + 1) * 128], rhs=hT[hp][:, k, :],
                                start=(k == 0), stop=(k == KD - 1)) for k in range(KD)]
                        else:
                            fns = []
                            for a in range(4):
                                for k in range(KD):
                                    fns.append(lambda e, k=k, a=a, cc=cc, bk=bk, hp=hp, W=W: e.matmul(
                                        psum[bk][:, a * 128:(a + 1) * 128], lhsT=hT[hp][:, k, a * 128:(a + 1) * 128],
                                        rhs=W[:, k, cc * 128:(cc + 1) * 128], start=(k == 0), stop=(k == KD - 1)))
                        S.group("pe", fns, reads=[B_hT[hp]] + B_W, writes=[B_ps[bk]])
                        on_tile(gi, tb, cc, bk)
            if post_group is not None:
                post_group(gi)

    def mixer_C(p):
        AR.reset()
        cw = AR.f32(HPC * 3)
        B_cw = Buf("cw")
        S.dma("sp", cw, p["convw"], writes=[B_cw])
        bg = [AR.f32(TB) for _ in range(2)]
        cg = [AR.f32(TB) for _ in range(2)]
        inner = [AR.f32(TB + 2) for _ in range(2)]
        sg = [AR.f32(TB) for _ in range(2)]
        tt = [AR.f32(TB) for _ in range(2)]
        yb = [AR.bf16(TB) for _ in range(2)]
        B_bg, B_cg, B_in, B_sg, B_tt, B_yb = ([Buf(n + "0"), Buf(n + "1")] for n in ("bg", "cg", "in", "sg", "tt", "yb"))
        colgroups = [dict(c0=j * 512, width=512, orient="FFFF") for j in range(HPC)]

        def on_tile(j, tb, cc, bk):
            q = tb % 2
            if cc == 0:
                S.op("act", lambda e: e.copy(out=bg[q], in_=psum[bk][:, :]), reads=[B_ps[bk]], writes=[B_bg[q]])
            elif cc == 1:
                S.op("act", lambda e: e.copy(out=cg[q], in_=psum[bk][:, :]), reads=[B_ps[bk]], writes=[B_cg[q]])
            elif cc == 2:
                if tb == 0:
                    S.op("dve", lambda e: e.memset(inner[q][:, 0:2], 0.0), writes=[B_in[q]])
                S.op("dve", lambda e: e.tensor_tensor(out=inner[q][:, 2:TB + 2], in0=psum[bk][:, :], in1=cg[q],
                                                      op=ALU.mult), reads=[B_ps[bk], B_cg[q]], writes=[B_in[q]])
                if tb + 1 < NTB:
                    S.op("act", lambda e: e.copy(out=inner[1 - q][:, 0:2], in_=inner[q][:, TB:TB + 2]),
                         reads=[B_in[q]], writes=[B_in[1 - q]])
            else:
                S.op("act", lambda e: e.activation(out=sg[q], in_=psum[bk][:, :], func=AF.Silu),
                     reads=[B_ps[bk]], writes=[B_sg[q]])
                S.op("dve", lambda e: e.tensor_scalar(out=tt[q], in0=inner[q][:, 0:TB], scalar1=cw[:, 3 * j:3 * j + 1],
                                                      scalar2=None, op0=ALU.mult),
                     reads=[B_in[q], B_cw], writes=[B_tt[q]])
                for tap in (1, 2):
                    S.op("dve", lambda e, tap=tap: e.scalar_tensor_tensor(
                        out=tt[q], in0=inner[q][:, tap:TB + tap], scalar=cw[:, 3 * j + tap:3 * j + tap + 1],
                        in1=tt[q], op0=ALU.mult, op1=ALU.add), reads=[B_in[q], B_cw, B_tt[q]], writes=[B_tt[q]])
                S.op("dve", lambda e: e.tensor_tensor(out=tt[q], in0=tt[q], in1=bg[q], op=ALU.mult),
                     reads=[B_tt[q], B_bg[q]], writes=[B_tt[q]])
                S.op("dve", lambda e: e.tensor_tensor(out=yb[q], in0=tt[q], in1=sg[q], op=ALU.mult),
                     reads=[B_tt[q], B_sg[q]], writes=[B_yb[q]])
                S.dma("pool", yT_part[j * 128:(j + 1) * 128, tb * TB:(tb + 1) * TB], yb[q],
                      reads=[B_yb[q]], writes=[B_yT_part])

        inproj(p["win"], colgroups, on_tile, nwbuf=2)
        S.barrier()

    def mixer_A1(p):
        AR.reset()
        gwv = min(512, EC)
        ncgv = EC // gwv
        sscols = AR.f32(ncgv * NTT)
        junk = AR.f32(512)
        vb = [AR.bf16(512) for _ in range(2)]
        B_ss, B_junk = Buf("sscols"), Buf("junk")
        B_vb = [Buf("vb0"), Buf("vb1")]
        S.op("dve", lambda e: e.memset(sscols, 0.0), writes=[B_ss])
        colgroups = [dict(c0=i * gwv, width=gwv, orient="TW") for i in range(ncgv)]
        cntv = [0]

        def on_tile_v(gi, tb, a, bk):
            q = cntv[0] % 2
            cntv[0] += 1
            col = gi * NTT + tb * 4 + a
            S.op("act", lambda e: e.activation(out=junk[:, 0:gwv], in_=psum[bk][:, 0:gwv], func=AF.Square),
                 reads=[B_ps[bk]], writes=[B_junk])
            S.op("dve", lambda e: e.reduce_sum(out=sscols[:, col:col + 1], in_=junk[:, 0:gwv], axis=mybir.AxisListType.X),
                 reads=[B_junk], writes=[B_ss])
            S.op("dve", lambda e: e.tensor_copy(out=vb[q][:, 0:gwv], in_=psum[bk][:, 0:gwv]),
                 reads=[B_ps[bk]], writes=[B_vb[q]])
            r0 = (tb * 4 + a) * 128
            S.dma("pool", vraw[r0:r0 + 128, gi * gwv:(gi + 1) * gwv], vb[q][:, 0:gwv], reads=[B_vb[q]], writes=[B_vraw])

        inproj(p["win"], colgroups, on_tile_v, nwbuf=2)
        ssv = AR.f32(NTT)
        B_ssv = Buf("ssv")
        S.op("dve", lambda e: e.tensor_copy(out=ssv, in_=sscols[:, 0:NTT]), reads=[B_ss], writes=[B_ssv])
        for i in range(1, ncgv):
            S.op("dve", lambda e, i=i: e.tensor_tensor(out=ssv, in0=ssv, in1=sscols[:, i * NTT:(i + 1) * NTT],
                                                        op=ALU.add), reads=[B_ss, B_ssv], writes=[B_ssv])
        S.dma("pool", ssv_part, ssv, reads=[B_ssv], writes=[B_ssv_part])
        S.barrier()

    def mixer_A2(p):
        AR.reset()
        rstdv = AR.f32(NTT)
        ssv8 = AR.f32(NCORES * NTT).rearrange("p (r t) -> p r t", r=NCORES)
        B_ssv8 = Buf("ssv8")
        vgain = AR.f32(HPC)
        wsTm = AR.f32(HPC * 128)
        biasr = AR.f32(HPC * 512)
        B_rv, B_vg, B_ws, B_bias = Buf("rstdv"), Buf("vgain"), Buf("wsTm"), Buf("biasr")
        S.dma("sp", ssv8, ssv_all.rearrange("(r p) t -> p r t", p=128), reads=[B_ssv_full], writes=[B_ssv8])
        S.op("dve", lambda e: e.tensor_copy(out=rstdv, in_=ssv8[:, 0, :]), reads=[B_ssv8], writes=[B_rv])
        for r in range(1, NCORES):
            S.op("dve", lambda e, r=r: e.tensor_tensor(out=rstdv, in0=rstdv, in1=ssv8[:, r, :], op=ALU.add),
                 reads=[B_ssv8, B_rv], writes=[B_rv])
        S.op("dve", lambda e: e.tensor_scalar(out=rstdv, in0=rstdv, scalar1=1.0 / E, scalar2=RMS_EPS,
                                              op0=ALU.mult, op1=ALU.add), reads=[B_rv], writes=[B_rv])
        S.op("act", lambda e: e.sqrt(out=rstdv, in_=rstdv), reads=[B_rv], writes=[B_rv])
        S.op("dve", lambda e: e.reciprocal(out=rstdv, in_=rstdv), reads=[B_rv], writes=[B_rv])
        S.dma("sp", vgain, p["vgain"], writes=[B_vg])
        S.dma("sp", wsTm, p["wsT"], writes=[B_ws])
        for g in range(HPC):
            S.op("dve", lambda e, g=g: e.tensor_tensor(out=wsTm[:, g * 128:(g + 1) * 128],
                                                        in0=wsTm[:, g * 128:(g + 1) * 128], in1=cmask_f, op=ALU.mult),
                 reads=[B_ws, B_const], writes=[B_ws])
        biasv = biasr.rearrange("p (g a t) -> p g a t", g=HPC, a=4)
        S.dma("sp", biasv[:, :, 0, :], p["wsb"].partition_broadcast(128).rearrange("p o (g t) -> p (o g) t", g=HPC),
              writes=[B_bias])
        for a in range(1, 4):
            S.op("dve", lambda e, a=a: e.tensor_copy(out=biasv[:, :, a, :], in_=biasv[:, :, 0, :]),
                 reads=[B_bias], writes=[B_bias])
        u = [[AR.f32(TB) for _ in range(2)] for _ in range(2)]
        sg = [[AR.f32(TB) for _ in range(2)] for _ in range(2)]
        vblk = [AR.bf16(4 * 256).rearrange("p (c x) -> p c x", c=4) for _ in range(2)]
        wss = [AR.bf16(128) for _ in range(4)]
        tmp = [AR.f32(TB) for _ in range(2)]
        yb = [AR.bf16(TB) for _ in range(2)]
        B_u = [[Buf("u"), Buf("u")] for _ in range(2)]
        B_sg = [[Buf("sg"), Buf("sg")] for _ in range(2)]
        B_vblk = [Buf("vblk0"), Buf("vblk1")]
        B_wss = [Buf(f"wss{i}") for i in range(4)]
        B_tmp = [Buf("tmp0"), Buf("tmp1")]
        B_yb = [Buf("yb0"), Buf("yb1")]
        c0 = EC
        colgroups = [dict(c0=c0 + pr * 512, width=512, orient="FFFF") for pr in range(HPC // 2)]
        wcnt = [0]
        ocnt = [0]

        def on_tile(pr, tb, cc, bk):
            q = tb % 2
            if cc == 0:
                src = vraw[tb * TB:(tb + 1) * TB, pr * 256:(pr + 1) * 256].rearrange("(c s) x -> s c x", s=128)
                S.dma("sp", vblk[q], src, reads=[B_vraw], writes=[B_vblk[q]])
            if cc < 2:
                S.op("act", lambda e: e.copy(out=u[q][cc], in_=psum[bk][:, :]), reads=[B_ps[bk]], writes=[B_u[q][cc]])
                return
            S.op("act", lambda e: e.activation(out=sg[q][cc - 2], in_=psum[bk][:, :], func=AF.Silu),
                 reads=[B_ps[bk]], writes=[B_sg[q][cc - 2]])
            if cc < 3:
                return
            for gi2 in range(2):
                g = 2 * pr + gi2
                b2 = 2 + (ocnt[0] % 2)
                o = ocnt[0] % 2
                ocnt[0] += 1
                for ch in range(4):
                    w = wcnt[0] % 4
                    wcnt[0] += 1
                    ti = tb * 4 + ch
                    S.op("dve", lambda e, w=w, ti=ti, g=g: e.tensor_scalar(
                        out=wss[w], in0=wsTm[:, g * 128:(g + 1) * 128], scalar1=rstdv[:, ti:ti + 1], scalar2=None,
                        op0=ALU.mult), reads=[B_ws, B_rv], writes=[B_wss[w]])
                    S.group("pe", [lambda e, w=w, ch=ch, gi2=gi2, b2=b2: e.matmul(
                        psum[b2][:, ch * 128:(ch + 1) * 128], lhsT=vblk[q][:, ch, gi2 * 128:(gi2 + 1) * 128],
                        rhs=wss[w], start=True, stop=True)], reads=[B_vblk[q], B_wss[w]], writes=[B_ps[b2]])
                S.op("dve", lambda e, g=g, b2=b2, o=o: e.scalar_tensor_tensor(
                    out=tmp[o], in0=psum[b2][:, :], scalar=vgain[:, g:g + 1], in1=biasr[:, g * 512:(g + 1) * 512],
                    op0=ALU.mult, op1=ALU.add), reads=[B_ps[b2], B_vg, B_bias], writes=[B_tmp[o]])
                S.op("dve", lambda e, o=o, gi2=gi2: e.tensor_tensor(out=tmp[o], in0=tmp[o], in1=u[q][gi2], op=ALU.mult),
                     reads=[B_tmp[o], B_u[q][gi2]], writes=[B_tmp[o]])
                S.op("dve", lambda e, o=o, gi2=gi2: e.tensor_tensor(out=yb[o], in0=tmp[o], in1=sg[q][gi2], op=ALU.mult),
                     reads=[B_tmp[o], B_sg[q][gi2]], writes=[B_yb[o]])
                S.dma("pool", yT_part[g * 128:(g + 1) * 128, tb * TB:(tb + 1) * TB], yb[o],
                      reads=[B_yb[o]], writes=[B_yT_part])

        inproj(p["win"], colgroups, on_tile, nwbuf=2)
        S.barrier()

    def mixer_B(p):
        AR.reset()
        NQT = NTB
        scale = 128.0 ** -0.5
        NH = NTT * HPC
        assert NH <= 512
        wf = AR.bf16(KD * HPC).rearrange("p (k h) -> p k h", k=KD)
        wf32 = AR.f32(KD * HPC).rearrange("p (k h) -> p k h", k=KD)
        B_wf32 = Buf("wf32")
        bft = AR.f32(HPC)
        cT = AR.f32(NH)
        cref = AR.f32(NH)
        mark = AR.off
        bfrep = AR.f32(NH)
        zt = AR.f32(NH)
        pp = [AR.f32(NH) for _ in range(2)]
        if AR.off - mark < 3 * (TB // 2) + 2 * TB + 2 * (TB // 2):
            AR.f32(3 * (TB // 2) + 2 * TB + 2 * (TB // 2) - (AR.off - mark))
        end1 = AR.off
        Bq = [AR.f32(4 * NTT).rearrange("p (q k) -> p q k", q=4) for _ in range(2)]
        B_Bq = [Buf("Bq0"), Buf("Bq1")]
        qT = AR.bf16(L)
        kT = AR.bf16(L)
        Vt = AR.bf16(L).rearrange("p (t d) -> p t d", d=128)
        sgT = AR.bf16(L)
        end2 = AR.off
        AR.off = mark
        pT = [AR.bf16(TB) for _ in range(3)]
        rs = AR.f32(TB)
        ot = AR.f32(TB)
        yb = [AR.bf16(TB) for _ in range(2)]
        assert AR.off <= end1
        AR.off = end2
        B_wf, B_bft, B_bfrep, B_zt, B_cT, B_cref, B_Bm = (Buf(n) for n in ("wf", "bft", "bfrep", "zt", "cT", "cref", "Bm_unused"))
        B_pp = [Buf("pp0"), Buf("pp1")]
        B_qT, B_kT, B_Vt, B_sgT = Buf("qT"), Buf("kT"), Buf("Vt"), Buf("sgT")
        B_pT = [Buf(f"pT{i}") for i in range(3)]
        B_rs, B_ot = Buf("rs"), Buf("ot")
        B_yb = [Buf("yb0"), Buf("yb1")]
        PSF = 7
        S.dma("sp", wf32, p["wf"].rearrange("(k p) h -> p k h", p=128), writes=[B_wf32])
        S.op("dve", lambda e: e.tensor_copy(out=wf, in_=wf32), reads=[B_wf32], writes=[B_wf])
        S.dma("sp", bft, p["bf"].partition_broadcast(128), writes=[B_bft])
        bfv = bfrep.rearrange("p (t h) -> p t h", h=HPC)
        S.op("dve", lambda e: e.tensor_copy(out=bfv[:, 0, :], in_=bft), reads=[B_bft], writes=[B_bfrep])
        d = 1
        while d < NTT:
            n = min(d, NTT - d)
            S.op("dve", lambda e, d=d, n=n: e.tensor_copy(out=bfv[:, d:d + n, :], in_=bfv[:, 0:n, :]),
                 reads=[B_bfrep], writes=[B_bfrep])
            d *= 2

        def extra_tb(tb, hTs, B_hTs):
            fns = []
            for a in range(4):
                ti = tb * 4 + a
                for k in range(KD):
                    fns.append(lambda e, k=k, a=a, ti=ti: e.matmul(
                        psum[PSF][:, ti * HPC:(ti + 1) * HPC], lhsT=hTs[:, k, a * 128:(a + 1) * 128], rhs=wf[:, k, :],
                        start=(k == 0), stop=(k == KD - 1)))
            S.group("pe", fns, reads=[B_hTs, B_wf], writes=[B_ps[PSF]])

        def forget_prep():
            S.op("dve", lambda e: e.tensor_tensor(out=zt, in0=psum[PSF][:, 0:NH], in1=bfrep, op=ALU.add),
                 reads=[B_ps[PSF], B_bfrep], writes=[B_zt])
            S.op("act", lambda e: e.activation(out=zt, in_=zt, func=AF.Exp, scale=-1.0), reads=[B_zt], writes=[B_zt])
            S.op("act", lambda e: e.activation(out=zt, in_=zt, func=AF.Ln, bias=1.0), reads=[B_zt], writes=[B_zt])
            S.op("dve", lambda e: e.tensor_scalar(out=zt, in0=zt, scalar1=-1.0, scalar2=None, op0=ALU.mult),
                 reads=[B_zt], writes=[B_zt])
            S.group("pe", [lambda e: e.matmul(psum[4][:, 0:NH], lhsT=cmask_f, rhs=zt, start=True, stop=True)],
                    reads=[B_zt, B_const], writes=[B_ps[4]])
            S.group("pe", [lambda e: e.matmul(psum[5][:, 0:NH], lhsT=ones_f, rhs=zt, start=True, stop=True)],
                    reads=[B_zt, B_const], writes=[B_ps[5]])
            S.op("dve", lambda e: e.tensor_copy(out=pp[0], in_=psum[5][:, 0:NH]), reads=[B_ps[5]], writes=[B_pp[0]])
            cur = 0
            d = 1
            while d < NTT:
                nxt = 1 - cur
                S.op("dve", lambda e, cur=cur, nxt=nxt, d=d: e.tensor_copy(out=pp[nxt][:, 0:d * HPC], in_=pp[cur][:, 0:d * HPC]),
                     reads=[B_pp[cur]], writes=[B_pp[nxt]])
                S.op("dve", lambda e, cur=cur, nxt=nxt, d=d: e.tensor_tensor(
                    out=pp[nxt][:, d * HPC:NH], in0=pp[cur][:, d * HPC:NH], in1=pp[cur][:, 0:NH - d * HPC], op=ALU.add),
                    reads=[B_pp[cur]], writes=[B_pp[nxt]])
                cur = nxt
                d *= 2
            S.op("dve", lambda e, cur=cur: e.tensor_copy(out=cref, in_=pp[cur]), reads=[B_pp[cur]], writes=[B_cref])
            S.op("dve", lambda e: e.tensor_tensor(out=cT, in0=cref, in1=psum[5][:, 0:NH], op=ALU.subtract),
                 reads=[B_cref, B_ps[5]], writes=[B_cT])
            S.op("dve", lambda e: e.tensor_tensor(out=cT, in0=cT, in1=psum[4][:, 0:NH], op=ALU.add),
                 reads=[B_cT, B_ps[4]], writes=[B_cT])
            S.barrier()

        colgroups = [dict(c0=h * 512, width=512, orient="FFTF") for h in range(HPC)]

        def on_tile(h, tb, cc, bk):
            sl = slice(tb * TB, (tb + 1) * TB)
            if cc == 0:
                S.op("act", lambda e: e.copy(out=qT[:, sl], in_=psum[bk][:, :]), reads=[B_ps[bk]], writes=[B_qT])
            elif cc == 1:
                S.op("dve", lambda e: e.tensor_copy(out=kT[:, sl], in_=psum[bk][:, :]), reads=[B_ps[bk]], writes=[B_kT])
            elif cc == 2:
                S.op("dve", lambda e: e.tensor_copy(out=Vt[:, tb * 4:tb * 4 + 4, :],
                                                    in_=psum[bk][:, :].rearrange("p (a d) -> p a d", d=128)),
                     reads=[B_ps[bk]], writes=[B_Vt])
            else:
                S.op("act", lambda e: e.activation(out=sgT[:, sl], in_=psum[bk][:, :], func=AF.Silu),
                     reads=[B_ps[bk]], writes=[B_sgT])

        crefv = cref.rearrange("p (t h) -> p t h", h=HPC)
        cTv = cT.rearrange("p (t h) -> p t h", h=HPC)

        def attention(h):
            if h == 0:
                forget_prep()
            steps = []
            for qt in range(NQT):
                for kt in range(4 * qt + 4):
                    steps.append((qt, kt))

            def issue_S(idx):
                qt, kt = steps[idx]
                i = kt - 4 * qt
                q0 = max(i, 0)
                bk = 2 + idx % 2
                S.group("pe", [lambda e: e.matmul(psum[bk][:, q0 * 128:TB], lhsT=kT[:, kt * 128:(kt + 1) * 128],
                                                   rhs=qT[:, qt * TB + q0 * 128:(qt + 1) * TB], start=True, stop=True)],
                        reads=[B_kT, B_qT], writes=[B_ps[bk]])

            def step(idx, qt, kt):
                i = kt - 4 * qt
                q0 = max(i, 0)
                bk = 2 + idx % 2
                sl = idx % 3
                par = qt % 2
                ob, sb = 4 + par, 6 + par
                nk = 4 * qt + 4
                if kt == 0:
                    for qs in range(4):
                        S.op("dve", lambda e, qs=qs: e.tensor_scalar(
                            out=Bq[par][:, qs, 0:nk], in0=cTv[:, 0:nk, h],
                            scalar1=crefv[:, 4 * qt + qs, h:h + 1], scalar2=-1.0, op0=ALU.subtract, op1=ALU.mult),
                            reads=[B_cref, B_cT], writes=[B_Bq[par]])
                for qs in range(q0, 4):
                    S.op("act", lambda e, qs=qs: e.activation(
                        out=pT[sl][:, qs * 128:(qs + 1) * 128], in_=psum[bk][:, qs * 128:(qs + 1) * 128], func=AF.Exp,
                        bias=Bq[par][:, qs, kt:kt + 1], scale=scale),
                        reads=[B_ps[bk], B_Bq[par]], writes=[B_pT[sl]])
                if i >= 0:
                    S.op("dve", lambda e: e.tensor_tensor(out=pT[sl][:, i * 128:(i + 1) * 128],
                                                          in0=pT[sl][:, i * 128:(i + 1) * 128], in1=cmask_b, op=ALU.mult),
                         reads=[B_pT[sl], B_const], writes=[B_pT[sl]])
                S.group("pe", [
                    lambda e: e.matmul(psum[ob][:, q0 * 128:TB], lhsT=Vt[:, kt, :], rhs=pT[sl][:, q0 * 128:TB],
                                       start=(kt == 0), stop=(kt == nk - 1), skip_group_check=True),
                    lambda e: e.matmul(psum[sb][:, q0 * 128:TB], lhsT=ones_b, rhs=pT[sl][:, q0 * 128:TB],
                                       start=(kt == 0), stop=(kt == nk - 1), skip_group_check=True)],
                    reads=[B_Vt, B_pT[sl], B_const], writes=[B_ps[ob], B_ps[sb]])
                if kt == nk - 1:
                    o = qt % 2
                    S.op("dve", lambda e: e.reciprocal(out=rs, in_=psum[sb][:, :]), reads=[B_ps[sb]], writes=[B_rs])
                    S.op("dve", lambda e: e.tensor_tensor(out=ot, in0=psum[ob][:, :], in1=rs, op=ALU.mult),
                         reads=[B_ps[ob], B_rs], writes=[B_ot])
                    S.op("dve", lambda e: e.tensor_tensor(out=yb[o], in0=ot, in1=sgT[:, qt * TB:(qt + 1) * TB], op=ALU.mult),
                         reads=[B_ot, B_sgT], writes=[B_yb[o]])
                    S.dma("pool", yT_part[h * 128:(h + 1) * 128, qt * TB:(qt + 1) * TB], yb[o],
                          reads=[B_yb[o]], writes=[B_yT_part])

            issue_S(0)
            for idx, (qt, kt) in enumerate(steps):
                if idx + 1 < len(steps):
                    issue_S(idx + 1)
                step(idx, qt, kt)

        inproj(p["win"], colgroups, on_tile, post_group=attention, extra_tb=extra_tb)
        S.barrier()

    def outproj(p):
        AR.reset()
        Wo = AR.bf16(KE * FCR).rearrange("p (k c) -> p k c", k=KE)
        NP = 4 if KE % 4 == 0 else 1
        KP = KE // NP
        NSL = 4
        ysl = [AR.bf16(KP * TB).rearrange("p (k t) -> p k t", k=KP) for _ in range(NSL)]
        xb = [AR.f32(FC * TB).rearrange("p (j t) -> p j t", j=FC) for _ in range(2)]
        xn = [AR.f32(FC * TB).rearrange("p (j t) -> p j t", j=FC) for _ in range(2)]
        KPS = min(4, KE)
        stg = [AR.f32(KPS * FCR).rearrange("p (k c) -> p k c", k=KPS) for _ in range(2)]
        B_stg = [Buf("ostg0"), Buf("ostg1")]
        B_Wo = Buf("Wo")
        B_ysl = [Buf(f"ysl{i}") for i in range(NSL)]
        B_xb = [Buf("xb0"), Buf("xb1")]
        B_xn = [Buf("xn0"), Buf("xn1")]
        for q in range(KE // KPS):
            sp_ = q % 2
            S.dma("sp", stg[sp_], p["wout"][:, q * KPS * FCR:(q + 1) * KPS * FCR].rearrange("p (k c) -> p k c", k=KPS),
                  writes=[B_stg[sp_]])
            S.op("dve", lambda e, q=q, sp_=sp_: e.tensor_copy(out=Wo[:, q * KPS:(q + 1) * KPS, :], in_=stg[sp_]),
                 reads=[B_stg[sp_]], writes=[B_Wo])
        scnt = 0
        for tb in range(NTB):
            par = tb % 2
            S.dma("sp", xb[par], xblk(xres, tb), reads=[B_xres[tb]], writes=[B_xb[par]])
            for pc in range(NP):
                sl = scnt % NSL
                scnt += 1
                k0, k1 = pc * KP, (pc + 1) * KP
                r0 = (tb * NP + pc) * 128
                S.dma("sp", ysl[sl], yT_full[r0:r0 + 128, :].rearrange("p (k t) -> p k t", k=KP),
                      reads=[B_yT_full], writes=[B_ysl[sl]])
                for cc in range(FC):
                    bk = par * 4 + cc
                    fns = [lambda e, k=k, cc=cc, bk=bk, sl=sl, pc=pc: e.matmul(
                        psum[bk][:, :], lhsT=Wo[:, pc * KP + k, cc * 128:(cc + 1) * 128], rhs=ysl[sl][:, k, :],
                        start=(pc == 0 and k == 0), stop=(pc == NP - 1 and k == KP - 1), skip_group_check=True)
                        for k in range(KP)]
                    S.group("pe", fns, reads=[B_Wo, B_ysl[sl]], writes=[B_ps[bk]])
            for cc in range(FC):
                bk = par * 4 + cc
                S.op("dve", lambda e, cc=cc, bk=bk, par=par: e.tensor_tensor(
                    out=xn[par][:, cc, :], in0=psum[bk][:, :], in1=xb[par][:, cc, :], op=ALU.add),
                    reads=[B_ps[bk], B_xb[par]], writes=[B_xn[par]])
            S.dma("pool", xblk(xout, tb), xn[par], reads=[B_xn[par]], writes=[B_xout[tb]])
        S.barrier()

    if kind == "ss0":
        sumsq_pass(xres, B_xres)
    elif kind == "norm":
        norm_stage(p["norm"], to_output=False)
    elif kind == "final":
        norm_stage(p["norm"], to_output=True)
    elif kind == "mixA1":
        mixer_A1(p)
    elif kind == "mixA2":
        mixer_A2(p)
    elif kind == "mixB":
        mixer_B(p)
    elif kind == "mixC":
        mixer_C(p)
    elif kind == "out":
        outproj(p)
        sumsq_pass(xout, B_xout)
    S.barrier()

    with nc.Block() as block:
        S.emit(block)
    stack.close()
    return nc


def _pp(vec, n):
    return np.ascontiguousarray(np.asarray(vec, np.float32).reshape(n, 128).T)


def layer_inputs(cfg, li, c, inp):
    D, E, EC, HPC, FC, FCR = cfg.D, cfg.E, cfg.EC, cfg.HPC, cfg.FC, cfg.FCR
    m = MIX[li]
    pre = f"l{li}_"
    d = {}
    d["gain"] = _pp(inp[pre + "norm"][c * FCR:(c + 1) * FCR], FC)
    w_in = inp[pre + "w_in"]
    ch = slice(c * EC, (c + 1) * EC)
    if m == "A":
        u = w_in[:, 0 * E:1 * E][:, ch]
        v = w_in[:, 1 * E:2 * E][:, ch]
        g = w_in[:, 2 * E:3 * E][:, ch]
        cols = [v]
        for pr in range(HPC // 2):
            a, b = 2 * pr, 2 * pr + 1
            cols += [u[:, a * 128:(a + 1) * 128], u[:, b * 128:(b + 1) * 128],
                     g[:, a * 128:(a + 1) * 128], g[:, b * 128:(b + 1) * 128]]
        gwv = min(512, EC)
        groups = [v[:, i * gwv:(i + 1) * gwv] for i in range(EC // gwv)]
        for pr in range(HPC // 2):
            groups.append(np.concatenate(cols[1 + 4 * pr:5 + 4 * pr], axis=1))
        d["win"] = _blk_w(groups)
        d["vgain"] = _pp(inp[pre + "v_gain"][ch], HPC)
        ws = inp[pre + "ws"][c * HPC:(c + 1) * HPC]
        d["wsT"] = np.ascontiguousarray(np.transpose(ws, (2, 0, 1)).reshape(128, HPC * 128))
        d["wsb"] = np.ascontiguousarray(inp[pre + "ws_bias"][c * HPC:(c + 1) * HPC].reshape(1, HPC * 128))
    else:
        parts = [w_in[:, j * E:(j + 1) * E][:, ch] for j in range(4)]
        cols = []
        for h in range(HPC):
            for j in range(4):
                cols.append(parts[j][:, h * 128:(h + 1) * 128])
        d["win"] = _blk_w([np.concatenate(cols[4 * h:4 * h + 4], axis=1) for h in range(HPC)])
        if m == "B":
            d["wf"] = np.ascontiguousarray(w_in[:, 4 * E + c * HPC:4 * E + (c + 1) * HPC])
            d["bf"] = np.ascontiguousarray(inp[pre + "b_f"][c * HPC:(c + 1) * HPC].reshape(1, HPC))
        else:
            cw = inp[pre + "conv_w"][:, ch]
            d["convw"] = np.ascontiguousarray(
                np.transpose(cw.reshape(3, HPC, 128), (2, 1, 0)).reshape(128, HPC * 3))
    d["wout"] = _blk_w([inp[pre + "w_out"][:, c * FCR:(c + 1) * FCR]])
    return d


def _blk_w(groups):
    out = []
    for g in groups:
        K = g.shape[0] // 128
        out.append(np.transpose(g.reshape(K, 128, g.shape[1]), (1, 0, 2)).reshape(128, K * g.shape[1]))
    return np.ascontiguousarray(np.concatenate(out, axis=1))


def _blk_hT(cfg, parts):
    full = np.concatenate(parts, axis=0)
    a = full.reshape(cfg.KD, 128, cfg.NTB, TB)
    return np.ascontiguousarray(np.transpose(a, (2, 1, 0, 3)).reshape(cfg.NTB * 128, cfg.KD * TB))


def _blk_yT(cfg, parts):
    full = np.concatenate(parts, axis=0)
    KP = cfg.KE // 4
    a = full.reshape(4, KP, 128, cfg.NTB, TB)
    return np.ascontiguousarray(np.transpose(a, (3, 0, 2, 1, 4)).reshape(cfg.NTB * 4 * 128, KP * TB))


_CONST = None


def consts():
    global _CONST
    if _CONST is None:
        _CONST = {"cmask": np.triu(np.ones((128, 128), np.float32)), "ident": np.eye(128, dtype=np.float32)}
    return _CONST


_PROGS = {}


def launch(cfg, kind, per_core):
    key = (cfg.L, cfg.D, cfg.E, kind)
    if key not in _PROGS:
        _PROGS[key] = build_segment(cfg, kind)
    nc = _PROGS[key]
    in_maps = []
    for c in range(NCORES):
        m = dict(per_core[c])
        m.update(consts())
        in_maps.append(m)
    res = run_bass_kernel_spmd(nc, in_maps, core_ids=list(range(NCORES)))
    return res.results


def run_forward(cfg, inp, layers=(0, 1, 2, 3)):
    inp = {k: np.asarray(v) for k, v in inp.items()}
    x = inp["x"][0]
    FCR = cfg.FCR
    C = range(NCORES)
    xT = [np.ascontiguousarray(x[:, c * FCR:(c + 1) * FCR].T) for c in C]
    r = launch(cfg, "ss0", [{"xT": xT[c]} for c in C])
    ss_all = np.ascontiguousarray(np.concatenate([r[c]["ss_part"] for c in C], axis=0))
    for li in layers:
        lp = [layer_inputs(cfg, li, c, inp) for c in C]
        r = launch(cfg, "norm", [{"xT": xT[c], "ss_all": ss_all, "gain": lp[c]["gain"]} for c in C])
        hT_full = _blk_hT(cfg, [r[c]["hT_part"] for c in C])
        m = MIX[li]
        if m == "A":
            r = launch(cfg, "mixA1", [{"hT_blk": hT_full, "win": lp[c]["win"]} for c in C])
            ssv_all = np.ascontiguousarray(np.concatenate([r[c]["ssv_part"] for c in C], axis=0))
            vraw = [r[c]["vraw"] for c in C]
            r = launch(cfg, "mixA2", [{"hT_blk": hT_full, "win": lp[c]["win"], "ssv_all": ssv_all, "vraw": vraw[c],
                                       "vgain": lp[c]["vgain"], "wsT": lp[c]["wsT"], "wsb": lp[c]["wsb"]} for c in C])
            del vraw
        elif m == "B":
            r = launch(cfg, "mixB", [{"hT_blk": hT_full, "win": lp[c]["win"], "wf": lp[c]["wf"], "bf": lp[c]["bf"]}
                                     for c in C])
        else:
            r = launch(cfg, "mixC", [{"hT_blk": hT_full, "win": lp[c]["win"], "convw": lp[c]["convw"]} for c in C])
        del hT_full
        yT_full = _blk_yT(cfg, [r[c]["yT_part"] for c in C])
        r = launch(cfg, "out", [{"yT_blk": yT_full, "xT": xT[c], "wout": lp[c]["wout"]} for c in C])
        del yT_full, lp
        xT = [np.ascontiguousarray(r[c]["xT_out"]) for c in C]
        ss_all = np.ascontiguousarray(np.concatenate([r[c]["ss_part"] for c in C], axis=0))
    r = launch(cfg, "final", [{"xT": xT[c], "ss_all": ss_all,
                               "gain": _pp(inp["final_norm"][c * FCR:(c + 1) * FCR], cfg.FC)} for c in C])
    out = np.concatenate([r[c]["out"] for c in C], axis=1)
    return out[None].astype(np.float32)


def kernel(**inputs):
    return run_forward(Cfg(), inputs)
```

```python
import numpy as np
import ml_dtypes
from contextlib import ExitStack
import concourse.bass as bass
import concourse.mybir as mybir
from concourse.bass_utils import run_bass_kernel_spmd

F32 = mybir.dt.float32
BF16 = mybir.dt.bfloat16
AF = mybir.ActivationFunctionType
ALU = mybir.AluOpType
NCORES = 8
RMS_EPS = 1e-6
TB = 512


class Cfg:
    def __init__(self, L=8192, D=4096, E=8192):
        self.L, self.D, self.E = L, D, E
        self.FC = D // NCORES // 128
        self.FCR = self.FC * 128
        self.HPC = E // NCORES // 128
        self.EC = self.HPC * 128
        self.KD = D // 128
        self.KE = E // 128
        self.NTB = L // TB
        self.NTT = L // 128


MIX = ("A", "B", "C", "A")


class Buf:
    __slots__ = ("name", "w", "r", "dsem")

    def __init__(self, name):
        self.name = name
        self.w = None
        self.r = {}
        self.dsem = None


class Sched:
    ENGS = ("pe", "act", "dve", "pool", "sp")

    def __init__(self, nc, stack):
        self.nc = nc
        self.stack = stack
        self.lists = {e: [] for e in self.ENGS}
        self.semh = {}
        self.semcnt = {}
        self.seen = {e: {} for e in self.ENGS}
        self.esem = {}
        for e in ("pe", "act", "dve", "pool"):
            self.esem[e] = self.new_sem("e_" + e)
        self.nsem = 0

    def new_sem(self, name):
        key = name + "_%d" % len(self.semh)
        self.semh[key] = self.stack.enter_context(self.nc.semaphore(key))
        self.semcnt[key] = 0
        return key

    def _waits(self, eng, toks):
        seen = self.seen[eng]
        own = self.esem.get(eng)
        for (s, v) in toks:
            if eng == "pe" and s == own:
                continue
            if seen.get(s, 0) >= v:
                continue
            seen[s] = v
            h = self.semh[s]
            self.lists[eng].append(lambda e, h=h, v=v: e.wait_ge(h, v))

    @staticmethod
    def _deps(reads, writes, deps):
        toks = list(deps)
        for b in reads:
            if b.w is not None:
                toks.append(b.w)
        for b in writes:
            if b.w is not None:
                toks.append(b.w)
            toks.extend(b.r.items())
        return toks

    @staticmethod
    def _mark(tok, reads, writes):
        s, v = tok
        for b in reads:
            if b.r.get(s, 0) < v:
                b.r[s] = v
        for b in writes:
            b.w = tok
            b.r = {}

    def op(self, eng, fn, reads=(), writes=(), deps=()):
        self._waits(eng, self._deps(reads, writes, deps))
        s = self.esem[eng]
        self.semcnt[s] += 1
        tok = (s, self.semcnt[s])
        h = self.semh[s]
        self.lists[eng].append(lambda e, fn=fn, h=h: fn(e).then_inc(h, 1))
        self._mark(tok, reads, writes)
        return tok

    def group(self, eng, fns, reads=(), writes=(), deps=()):
        self._waits(eng, self._deps(reads, writes, deps))
        s = self.esem[eng]
        self.semcnt[s] += 1
        tok = (s, self.semcnt[s])
        h = self.semh[s]
        lst = self.lists[eng]
        for fn in fns[:-1]:
            lst.append(fn)
        lst.append(lambda e, fn=fns[-1], h=h: fn(e).then_inc(h, 1))
        self._mark(tok, reads, writes)
        return tok

    def dma(self, q, out_ap, in_ap, reads=(), writes=(), deps=(), sembuf=None):
        sb = sembuf if sembuf is not None else writes[0]
        if sb.dsem is None:
            sb.dsem = self.new_sem("d_" + sb.name)
        s = sb.dsem
        self._waits(q, self._deps(reads, writes, deps))
        self.semcnt[s] += 16
        tok = (s, self.semcnt[s])
        h = self.semh[s]
        self.lists[q].append(lambda e, o=out_ap, i=in_ap, h=h: e.dma_start(out=o, in_=i).then_inc(h, 16))
        self._mark(tok, reads, writes)
        return tok

    def collective(self, kind, op, in_ap, out_ap, reads=(), writes=()):
        import os
        if os.environ.get("NOCC"):
            n = in_ap.shape[0]
            return self.dma("sp", out_ap[0:n, :], in_ap, reads=reads, writes=writes)
        s = self.new_sem("cc")
        self._waits("pool", self._deps(reads, writes, ()))
        self.semcnt[s] = 1
        tok = (s, 1)
        h = self.semh[s]
        self.lists["pool"].append(
            lambda e, h=h: e.collective_compute(kind, op, replica_groups=[list(range(NCORES))],
                                                ins=[in_ap], outs=[out_ap]).then_inc(h, 1))
        self._mark(tok, reads, writes)
        self._waits("pool", [tok])
        return tok

    def barrier(self):
        toks = [(s, v) for s, v in self.semcnt.items() if v > 0]
        for e in self.ENGS:
            self._waits(e, toks)

    def emit(self, block):
        nc = self.nc
        L = self.lists

        @block.tensor
        def _(e):
            for f in L["pe"]:
                f(e)

        @block.scalar
        def _(e):
            for f in L["act"]:
                f(e)

        @block.vector
        def _(e):
            for f in L["dve"]:
                f(e)

        @block.gpsimd
        def _(e):
            for f in L["pool"]:
                f(e)

        @block.sync
        def _(e):
            for f in L["sp"]:
                f(e)


class Arena:
    def __init__(self, t, words):
        self.t = t
        self.words = words
        self.off = 0

    def reset(self):
        self.off = 0

    def f32(self, n):
        assert self.off + n <= self.words, ("SBUF arena overflow", self.off, n, self.words)
        ap = self.t[:, self.off:self.off + n]
        self.off += n
        return ap

    def bf16(self, n):
        w = (n + 1) // 2
        return self.f32(w).bitcast(BF16)


def build_segment(cfg, kind):
    L, D, E = cfg.L, cfg.D, cfg.E
    FC, FCR, HPC, EC, KD, KE, NTB, NTT = cfg.FC, cfg.FCR, cfg.HPC, cfg.EC, cfg.KD, cfg.KE, cfg.NTB, cfg.NTT
    nc = bass.Bass("TRN2", target_bir_lowering=False)
    stack = ExitStack()
    S = Sched(nc, stack)

    def ext_in(name, shape, dt=F32):
        return nc.dram_tensor(name, list(shape), dt, kind="ExternalInput").ap()

    def ext_out(name, shape, dt=F32):
        return nc.dram_tensor(name, list(shape), dt, kind="ExternalOutput").ap()

    cmask_in = ext_in("cmask", [128, 128])
    sel_in = ext_in("sel", [128, 2])
    ident_in = ext_in("ident", [128, 128])
    p = {}
    xres = xout = hT_part = hT_full = yT_part = yT_full = ss_part = ss_all = ssv_part = ssv_all = vraw = out_ext = None
    if kind == "ss0":
        xres = ext_in("xT", [FCR, L])
        ss_part = ext_out("ss_part", [1, L])
    elif kind in ("norm", "final"):
        xres = ext_in("xT", [FCR, L])
        ss_all = ext_in("ss_all", [NCORES, L])
        p["norm"] = ext_in("gain", [128, FC])
        if kind == "norm":
            hT_part = ext_out("hT_part", [FCR, L], BF16)
        else:
            out_ext = ext_out("out", [L, FCR])
    elif kind.startswith("mix"):
        hT_full = ext_in("hT_blk", [NTB * 128, KD * TB], BF16)
        m = kind[3]
        p["win"] = ext_in("win", [128, KD * (3 * EC if m == "A" else 4 * EC)])
        if kind == "mixA1":
            ssv_part = ext_out("ssv_part", [128, NTT])
            vraw = ext_out("vraw", [L, EC], BF16)
        else:
            yT_part = ext_out("yT_part", [EC, L], BF16)
        if kind == "mixA2":
            ssv_all = ext_in("ssv_all", [NCORES * 128, NTT])
            vraw = ext_in("vraw", [L, EC], BF16)
            p["vgain"] = ext_in("vgain", [128, HPC])
            p["wsT"] = ext_in("wsT", [128, HPC * 128])
            p["wsb"] = ext_in("wsb", [1, HPC * 128])
        elif kind == "mixB":
            p["wf"] = ext_in("wf", [D, HPC])
            p["bf"] = ext_in("bf", [1, HPC])
        elif kind == "mixC":
            p["convw"] = ext_in("convw", [128, HPC * 3])
    elif kind == "out":
        yT_full = ext_in("yT_blk", [NTB * 4 * 128, (KE // 4) * TB], BF16)
        xres = ext_in("xT", [FCR, L])
        p["wout"] = ext_in("wout", [128, KE * FCR])
        xout = ext_out("xT_out", [FCR, L])
        ss_part = ext_out("ss_part", [1, L])
    else:
        raise ValueError(kind)

    B_xres = [Buf(f"xres{tb}") for tb in range(NTB)]
    B_xout = [Buf(f"xout{tb}") for tb in range(NTB)]
    B_hT_part, B_hT_full = Buf("hT_part"), Buf("hT_full")
    B_yT_part, B_yT_full = Buf("yT_part"), Buf("yT_full")
    B_ss_part, B_ss_full = Buf("ss_part"), Buf("ss_full")
    B_ssv_part, B_ssv_full = Buf("ssv_part"), Buf("ssv_full")
    B_vraw = Buf("vraw")

    ARENA_WORDS = 51200
    arena_t = stack.enter_context(nc.sbuf_tensor("arena", [128, ARENA_WORDS], F32))
    const_t = stack.enter_context(nc.sbuf_tensor("consts", [128, 1024], F32))
    AR = Arena(arena_t, ARENA_WORDS)
    psum = [stack.enter_context(nc.psum_tensor(f"ps{i}", [128, 512], F32)) for i in range(8)]
    B_ps = [Buf(f"ps{i}") for i in range(8)]

    cmask_f = const_t[:, 0:128]
    ident_f = const_t[:, 128:256]
    ones_f = const_t[:, 256:384]
    cmask_b = const_t[:, 384:448].bitcast(BF16)
    ones_b = const_t[:, 448:512].bitcast(BF16)
    B_const = Buf("const")
    S.dma("sp", cmask_f, cmask_in, writes=[B_const])
    S.dma("sp", ident_f, ident_in, writes=[B_const])
    sel_f = const_t[:, 512:514]
    S.dma("sp", sel_f, sel_in, writes=[B_const])
    S.op("dve", lambda e: e.memset(ones_f, 1.0), writes=[B_const])
    S.op("dve", lambda e: e.memset(ones_b, 1.0), writes=[B_const])
    S.op("dve", lambda e: e.tensor_copy(out=cmask_b, in_=cmask_f), reads=[B_const], writes=[B_const])
    S.barrier()

    def xblk(src, tb):
        return src[:, tb * TB:(tb + 1) * TB].rearrange("(j p) t -> p j t", p=128)

    def sumsq_pass(src, B_src):
        AR.reset()
        xb = [AR.f32(FC * TB).rearrange("p (j t) -> p j t", j=FC) for _ in range(2)]
        sq = [AR.f32(FC * TB).rearrange("p (j t) -> p j t", j=FC) for _ in range(2)]
        ssb = [AR.f32(TB) for _ in range(2)]
        B_xb = [Buf("sxb0"), Buf("sxb1")]
        B_sq = [Buf("sq0"), Buf("sq1")]
        B_ssb = [Buf("ssb0"), Buf("ssb1")]
        for tb in range(NTB):
            p_ = tb % 2
            S.dma("sp", xb[p_], xblk(src, tb), reads=[B_src[tb]], writes=[B_xb[p_]])
            S.op("act", lambda e, p_=p_: e.activation(out=sq[p_], in_=xb[p_], func=AF.Square),
                 reads=[B_xb[p_]], writes=[B_sq[p_]])
            fns = []
            for j in range(FC):
                fns.append(lambda e, p_=p_, j=j: e.matmul(psum[p_][:, :], lhsT=ones_f, rhs=sq[p_][:, j, :],
                                                           start=(j == 0), stop=(j == FC - 1)))
            S.group("pe", fns, reads=[B_sq[p_], B_const], writes=[B_ps[p_]])
            S.op("dve", lambda e, p_=p_: e.tensor_copy(out=ssb[p_][0:1, :], in_=psum[p_][0:1, :]),
                 reads=[B_ps[p_]], writes=[B_ssb[p_]])
            S.dma("pool", ss_part[:, tb * TB:(tb + 1) * TB], ssb[p_][0:1, :], reads=[B_ssb[p_]], writes=[B_ss_part])
        S.barrier()

    def norm_stage(gain_in, to_output):
        AR.reset()
        rstd_t = AR.f32(L)
        rtmp = AR.f32(L)
        gain_t = AR.f32(FC)
        xb = [AR.f32(FC * TB).rearrange("p (j t) -> p j t", j=FC) for _ in range(2)]
        B_rstd, B_gain, B_rtmp = Buf("rstd"), Buf("gain"), Buf("rtmp")
        B_xb = [Buf("xb0"), Buf("xb1")]
        S.dma("sp", gain_t, gain_in, writes=[B_gain])
        S.dma("sp", rstd_t, ss_all[0:1, :].partition_broadcast(128), reads=[B_ss_full], writes=[B_rstd])
        for r in range(1, NCORES):
            S.dma("sp", rtmp, ss_all[r:r + 1, :].partition_broadcast(128), reads=[B_ss_full], writes=[B_rtmp])
            S.op("dve", lambda e: e.tensor_tensor(out=rstd_t, in0=rstd_t, in1=rtmp, op=ALU.add),
                 reads=[B_rstd, B_rtmp], writes=[B_rstd])
        S.op("dve", lambda e: e.tensor_scalar(out=rstd_t, in0=rstd_t, scalar1=1.0 / D, scalar2=RMS_EPS,
                                              op0=ALU.mult, op1=ALU.add), reads=[B_rstd], writes=[B_rstd])
        S.op("act", lambda e: e.sqrt(out=rstd_t, in_=rstd_t), reads=[B_rstd], writes=[B_rstd])
        S.op("dve", lambda e: e.reciprocal(out=rstd_t, in_=rstd_t), reads=[B_rstd], writes=[B_rstd])
        if not to_output:
            hb = [AR.bf16(FC * TB).rearrange("p (j t) -> p j t", j=FC) for _ in range(2)]
            B_hb = [Buf("hb0"), Buf("hb1")]
            for tb in range(NTB):
                p = tb % 2
                S.dma("sp", xb[p], xblk(xres, tb), reads=[B_xres[tb]], writes=[B_xb[p]])
                for j in range(FC):
                    S.op("dve", lambda e, p=p, j=j, tb=tb: e.scalar_tensor_tensor(
                        out=hb[p][:, j, :], in0=xb[p][:, j, :], scalar=gain_t[:, j:j + 1],
                        in1=rstd_t[:, tb * TB:(tb + 1) * TB], op0=ALU.mult, op1=ALU.mult),
                        reads=[B_xb[p], B_rstd, B_gain], writes=[B_hb[p]])
                S.dma("pool", xblk(hT_part, tb), hb[p], reads=[B_hb[p]], writes=[B_hT_part])
        else:
            hf = [AR.f32(FC * TB).rearrange("p (j t) -> p j t", j=FC) for _ in range(2)]
            ob = [AR.f32(4 * FCR).rearrange("p (a c) -> p a c", a=4) for _ in range(2)]
            B_hf = [Buf("hf0"), Buf("hf1")]
            B_ob = [Buf("ob0"), Buf("ob1")]
            B_out = Buf("out")
            for tb in range(NTB):
                p = tb % 2
                S.dma("sp", xb[p], xblk(xres, tb), reads=[B_xres[tb]], writes=[B_xb[p]])
                for j in range(FC):
                    S.op("dve", lambda e, p=p, j=j, tb=tb: e.scalar_tensor_tensor(
                        out=hf[p][:, j, :], in0=xb[p][:, j, :], scalar=gain_t[:, j:j + 1],
                        in1=rstd_t[:, tb * TB:(tb + 1) * TB], op0=ALU.mult, op1=ALU.mult),
                        reads=[B_xb[p], B_rstd, B_gain], writes=[B_hf[p]])
                for a in range(4):
                    bk = (tb * 4 + a) % 4
                    fns = []
                    for j in range(FC):
                        fns.append(lambda e, p=p, a=a, j=j, bk=bk: e.transpose(
                            psum[bk][:, j * 128:(j + 1) * 128], hf[p][:, j, a * 128:(a + 1) * 128], ident_f))
                    S.group("pe", fns, reads=[B_hf[p], B_const], writes=[B_ps[bk]])
                    S.op("act", lambda e, p=p, a=a, bk=bk: e.copy(out=ob[p][:, a, :], in_=psum[bk][:, 0:FCR]),
                         reads=[B_ps[bk]], writes=[B_ob[p]])
                S.dma("pool", out_ext[tb * TB:(tb + 1) * TB, :].rearrange("(a p) c -> p a c", p=128), ob[p],
                      reads=[B_ob[p]], writes=[B_out])
        S.barrier()

    def inproj(win, colgroups, on_tile, pre_group=None, post_group=None, extra_tb=None, ps_banks=(0, 1), nwbuf=1):
        hT = [AR.bf16(KD * TB).rearrange("p (k t) -> p k t", k=KD) for _ in range(2)]
        B_hT = [Buf("hT0"), Buf("hT1")]
        gwmax = max(g["width"] for g in colgroups)
        Wts = [AR.bf16(KD * gwmax) for _ in range(nwbuf)]
        KPS = min(4, KD)
        NPIECE = KD // KPS
        stg = [AR.f32(KPS * gwmax) for _ in range(2)]
        B_stg = [Buf("stg0"), Buf("stg1")]
        B_Ws = [[Buf(f"W{b}_{q}") for q in range(NPIECE)] for b in range(nwbuf)]
        cnt = 0
        scnt = [0]

        def load_W(gi):
            g = colgroups[gi]
            gw = g["width"]
            W = Wts[gi % nwbuf][:, 0:KD * gw].rearrange("p (k c) -> p k c", k=KD)
            for q in range(NPIECE):
                sp_ = scnt[0] % 2
                scnt[0] += 1
                st = stg[sp_][:, 0:KPS * gw].rearrange("p (k c) -> p k c", k=KPS)
                o0 = KD * g["c0"] + q * KPS * gw
                src = win[:, o0:o0 + KPS * gw].rearrange("p (k c) -> p k c", k=KPS)
                S.dma("sp", st, src, writes=[B_stg[sp_]])
                S.op("dve", lambda e, st=st, W=W, q=q: e.tensor_copy(out=W[:, q * KPS:(q + 1) * KPS, :], in_=st),
                     reads=[B_stg[sp_]], writes=[B_Ws[gi % nwbuf][q]])

        load_W(0)
        for gi, g in enumerate(colgroups):
            gw = g["width"]
            W = Wts[gi % nwbuf][:, 0:KD * gw].rearrange("p (k c) -> p k c", k=KD)
            B_W = B_Ws[gi % nwbuf]
            if nwbuf == 1 and gi > 0:
                load_W(gi)
            if pre_group is not None:
                pre_group(gi)
            for tb in range(NTB):
                if nwbuf == 2 and tb == NTB // 2 and gi + 1 < len(colgroups):
                    load_W(gi + 1)
                hp = (gi * NTB + tb) % 2
                HQ = 4 if KD % 4 == 0 else 1
                KQ = KD // HQ
                for q in range(HQ):
                    src = hT_full[tb * 128:(tb + 1) * 128, q * KQ * TB:(q + 1) * KQ * TB].rearrange(
                        "p (k t) -> p k t", k=KQ)
                    S.dma("sp", hT[hp][:, q * KQ:(q + 1) * KQ, :], src, reads=[B_hT_full], writes=[B_hT[hp]])
                if extra_tb is not None and gi == 0:
                    extra_tb(tb, hT[hp], B_hT[hp])
                if g["orient"] == "TW":
                    for a in range(4):
                        bk = ps_banks[cnt % 2]
                        cnt += 1
                        fns = [lambda e, k=k, a=a, bk=bk, hp=hp, W=W, gw=gw: e.matmul(
                            psum[bk][:, 0:gw], lhsT=hT[hp][:, k, a * 128:(a + 1) * 128], rhs=W[:, k, :],
                            start=(k == 0), stop=(k == KD - 1)) for k in range(KD)]
                        S.group("pe", fns, reads=[B_hT[hp]] + B_W, writes=[B_ps[bk]])
                        on_tile(gi, tb, a, bk)
                else:
                    for cc, orient in enumerate(g["orient"]):
                        bk = ps_banks[cnt % 2]
                        cnt += 1
                        if orient == "F":
                            fns = [lambda e, k=k, cc=cc, bk=bk, hp=hp, W=W: e.matmul(
                                psum[bk][:, :], lhsT=W[:, k, cc * 128:(cc + 1) * 128], rhs=hT[hp][:, k, :],
                                start=(k == 0), stop=(k == KD - 1)) for k in range(KD)]
                        else:
                            fns = []
                            for a in range(4):
                                for k in range(KD):
                                    fns.append(lambda e, k=k, a=a, cc=cc, bk=bk, hp=hp, W=W: e.matmul(
                                        psum[bk][:, a * 128:(a + 1) * 128], lhsT=hT[hp][:, k, a * 128:(a + 1) * 128],
                                        rhs=W[:, k, cc * 128:(cc + 1) * 128], start=(k == 0), stop=(k == KD - 1)))
                        S.group("pe", fns, reads=[B_hT[hp]] + B_W, writes=[B_ps[bk]])
                        on_tile(gi, tb, cc, bk)
            if post_group is not None:
                post_group(gi)

    def mixer_C(p):
        AR.reset()
        cw = AR.f32(HPC * 3)
        B_cw = Buf("cw")
        S.dma("sp", cw, p["convw"], writes=[B_cw])
        bg = [AR.f32(TB) for _ in range(2)]
        cg = [AR.f32(TB) for _ in range(2)]
        inner = [AR.f32(TB + 2) for _ in range(2)]
        sg = [AR.f32(TB) for _ in range(2)]
        tt = [AR.f32(TB) for _ in range(2)]
        yb = [AR.bf16(TB) for _ in range(2)]
        B_bg, B_cg, B_in, B_sg, B_tt, B_yb = ([Buf(n + "0"), Buf(n + "1")] for n in ("bg", "cg", "in", "sg", "tt", "yb"))
        colgroups = [dict(c0=j * 512, width=512, orient="FFFF") for j in range(HPC)]

        def on_tile(j, tb, cc, bk):
            q = tb % 2
            if cc == 0:
                S.op("act", lambda e: e.copy(out=bg[q], in_=psum[bk][:, :]), reads=[B_ps[bk]], writes=[B_bg[q]])
            elif cc == 1:
                S.op("act", lambda e: e.copy(out=cg[q], in_=psum[bk][:, :]), reads=[B_ps[bk]], writes=[B_cg[q]])
            elif cc == 2:
                if tb == 0:
                    S.op("dve", lambda e: e.memset(inner[q][:, 0:2], 0.0), writes=[B_in[q]])
                S.op("dve", lambda e: e.tensor_tensor(out=inner[q][:, 2:TB + 2], in0=psum[bk][:, :], in1=cg[q],
                                                      op=ALU.mult), reads=[B_ps[bk], B_cg[q]], writes=[B_in[q]])
                if tb + 1 < NTB:
                    S.op("act", lambda e: e.copy(out=inner[1 - q][:, 0:2], in_=inner[q][:, TB:TB + 2]),
                         reads=[B_in[q]], writes=[B_in[1 - q]])
            else:
                S.op("act", lambda e: e.activation(out=sg[q], in_=psum[bk][:, :], func=AF.Silu),
                     reads=[B_ps[bk]], writes=[B_sg[q]])
                S.op("dve", lambda e: e.tensor_scalar(out=tt[q], in0=inner[q][:, 0:TB], scalar1=cw[:, 3 * j:3 * j + 1],
                                                      scalar2=None, op0=ALU.mult),
                     reads=[B_in[q], B_cw], writes=[B_tt[q]])
                for tap in (1, 2):
                    S.op("dve", lambda e, tap=tap: e.scalar_tensor_tensor(
                        out=tt[q], in0=inner[q][:, tap:TB + tap], scalar=cw[:, 3 * j + tap:3 * j + tap + 1],
                        in1=tt[q], op0=ALU.mult, op1=ALU.add), reads=[B_in[q], B_cw, B_tt[q]], writes=[B_tt[q]])
                S.op("dve", lambda e: e.tensor_tensor(out=tt[q], in0=tt[q], in1=bg[q], op=ALU.mult),
                     reads=[B_tt[q], B_bg[q]], writes=[B_tt[q]])
                S.op("dve", lambda e: e.tensor_tensor(out=yb[q], in0=tt[q], in1=sg[q], op=ALU.mult),
                     reads=[B_tt[q], B_sg[q]], writes=[B_yb[q]])
                S.dma("pool", yT_part[j * 128:(j + 1) * 128, tb * TB:(tb + 1) * TB], yb[q],
                      reads=[B_yb[q]], writes=[B_yT_part])

        inproj(p["win"], colgroups, on_tile, nwbuf=2)
        S.barrier()

    def mixer_A1(p):
        AR.reset()
        gwv = min(512, EC)
        ncgv = EC // gwv
        sscols = AR.f32(ncgv * NTT)
        junk = AR.f32(512)
        vb = [AR.bf16(512) for _ in range(2)]
        B_ss, B_junk = Buf("sscols"), Buf("junk")
        B_vb = [Buf("vb0"), Buf("vb1")]
        S.op("dve", lambda e: e.memset(sscols, 0.0), writes=[B_ss])
        colgroups = [dict(c0=i * gwv, width=gwv, orient="TW") for i in range(ncgv)]
        cntv = [0]

        def on_tile_v(gi, tb, a, bk):
            q = cntv[0] % 2
            cntv[0] += 1
            col = gi * NTT + tb * 4 + a
            S.op("act", lambda e: e.activation(out=junk[:, 0:gwv], in_=psum[bk][:, 0:gwv], func=AF.Square),
                 reads=[B_ps[bk]], writes=[B_junk])
            S.op("dve", lambda e: e.reduce_sum(out=sscols[:, col:col + 1], in_=junk[:, 0:gwv], axis=mybir.AxisListType.X),
                 reads=[B_junk], writes=[B_ss])
            S.op("dve", lambda e: e.tensor_copy(out=vb[q][:, 0:gwv], in_=psum[bk][:, 0:gwv]),
                 reads=[B_ps[bk]], writes=[B_vb[q]])
            r0 = (tb * 4 + a) * 128
            S.dma("pool", vraw[r0:r0 + 128, gi * gwv:(gi + 1) * gwv], vb[q][:, 0:gwv], reads=[B_vb[q]], writes=[B_vraw])

        inproj(p["win"], colgroups, on_tile_v, nwbuf=2)
        ssv = AR.f32(NTT)
        B_ssv = Buf("ssv")
        S.op("dve", lambda e: e.tensor_copy(out=ssv, in_=sscols[:, 0:NTT]), reads=[B_ss], writes=[B_ssv])
        for i in range(1, ncgv):
            S.op("dve", lambda e, i=i: e.tensor_tensor(out=ssv, in0=ssv, in1=sscols[:, i * NTT:(i + 1) * NTT],
                                                        op=ALU.add), reads=[B_ss, B_ssv], writes=[B_ssv])
        S.dma("pool", ssv_part, ssv, reads=[B_ssv], writes=[B_ssv_part])
        S.barrier()

    def mixer_A2(p):
        AR.reset()
        rstdv = AR.f32(NTT)
        ssv8 = AR.f32(NCORES * NTT).rearrange("p (r t) -> p r t", r=NCORES)
        B_ssv8 = Buf("ssv8")
        vgain = AR.f32(HPC)
        wsTm = AR.f32(HPC * 128)
        biasr = AR.f32(HPC * 512)
        B_rv, B_vg, B_ws, B_bias = Buf("rstdv"), Buf("vgain"), Buf("wsTm"), Buf("biasr")
        S.dma("sp", ssv8, ssv_all.rearrange("(r p) t -> p r t", p=128), reads=[B_ssv_full], writes=[B_ssv8])
        S.op("dve", lambda e: e.tensor_copy(out=rstdv, in_=ssv8[:, 0, :]), reads=[B_ssv8], writes=[B_rv])
        for r in range(1, NCORES):
            S.op("dve", lambda e, r=r: e.tensor_tensor(out=rstdv, in0=rstdv, in1=ssv8[:, r, :], op=ALU.add),
                 reads=[B_ssv8, B_rv], writes=[B_rv])
        S.op("dve", lambda e: e.tensor_scalar(out=rstdv, in0=rstdv, scalar1=1.0 / E, scalar2=RMS_EPS,
                                              op0=ALU.mult, op1=ALU.add), reads=[B_rv], writes=[B_rv])
        S.op("act", lambda e: e.sqrt(out=rstdv, in_=rstdv), reads=[B_rv], writes=[B_rv])
        S.op("dve", lambda e: e.reciprocal(out=rstdv, in_=rstdv), reads=[B_rv], writes=[B_rv])
        S.dma("sp", vgain, p["vgain"], writes=[B_vg])
        S.dma("sp", wsTm, p["wsT"], writes=[B_ws])
        for g in range(HPC):
            S.op("dve", lambda e, g=g: e.tensor_tensor(out=wsTm[:, g * 128:(g + 1) * 128],
                                                        in0=wsTm[:, g * 128:(g + 1) * 128], in1=cmask_f, op=ALU.mult),
                 reads=[B_ws, B_const], writes=[B_ws])
        biasv = biasr.rearrange("p (g a t) -> p g a t", g=HPC, a=4)
        S.dma("sp", biasv[:, :, 0, :], p["wsb"].partition_broadcast(128).rearrange("p o (g t) -> p (o g) t", g=HPC),
              writes=[B_bias])
        for a in range(1, 4):
            S.op("dve", lambda e, a=a: e.tensor_copy(out=biasv[:, :, a, :], in_=biasv[:, :, 0, :]),
                 reads=[B_bias], writes=[B_bias])
        u = [[AR.f32(TB) for _ in range(2)] for _ in range(2)]
        sg = [[AR.f32(TB) for _ in range(2)] for _ in range(2)]
        vblk = [AR.bf16(4 * 256).rearrange("p (c x) -> p c x", c=4) for _ in range(2)]
        wss = [AR.bf16(128) for _ in range(4)]
        tmp = [AR.f32(TB) for _ in range(2)]
        yb = [AR.bf16(TB) for _ in range(2)]
        B_u = [[Buf("u"), Buf("u")] for _ in range(2)]
        B_sg = [[Buf("sg"), Buf("sg")] for _ in range(2)]
        B_vblk = [Buf("vblk0"), Buf("vblk1")]
        B_wss = [Buf(f"wss{i}") for i in range(4)]
        B_tmp = [Buf("tmp0"), Buf("tmp1")]
        B_yb = [Buf("yb0"), Buf("yb1")]
        c0 = EC
        colgroups = [dict(c0=c0 + pr * 512, width=512, orient="FFFF") for pr in range(HPC // 2)]
        wcnt = [0]
        ocnt = [0]

        def on_tile(pr, tb, cc, bk):
            q = tb % 2
            if cc == 0:
                src = vraw[tb * TB:(tb + 1) * TB, pr * 256:(pr + 1) * 256].rearrange("(c s) x -> s c x", s=128)
                S.dma("sp", vblk[q], src, reads=[B_vraw], writes=[B_vblk[q]])
            if cc < 2:
                S.op("act", lambda e: e.copy(out=u[q][cc], in_=psum[bk][:, :]), reads=[B_ps[bk]], writes=[B_u[q][cc]])
                return
            S.op("act", lambda e: e.activation(out=sg[q][cc - 2], in_=psum[bk][:, :], func=AF.Silu),
                 reads=[B_ps[bk]], writes=[B_sg[q][cc - 2]])
            if cc < 3:
                return
            for gi2 in range(2):
                g = 2 * pr + gi2
                b2 = 2 + (ocnt[0] % 2)
                o = ocnt[0] % 2
                ocnt[0] += 1
                for ch in range(4):
                    w = wcnt[0] % 4
                    wcnt[0] += 1
                    ti = tb * 4 + ch
                    S.op("dve", lambda e, w=w, ti=ti, g=g: e.tensor_scalar(
                        out=wss[w], in0=wsTm[:, g * 128:(g + 1) * 128], scalar1=rstdv[:, ti:ti + 1], scalar2=None,
                        op0=ALU.mult), reads=[B_ws, B_rv], writes=[B_wss[w]])
                    S.group("pe", [lambda e, w=w, ch=ch, gi2=gi2, b2=b2: e.matmul(
                        psum[b2][:, ch * 128:(ch + 1) * 128], lhsT=vblk[q][:, ch, gi2 * 128:(gi2 + 1) * 128],
                        rhs=wss[w], start=True, stop=True)], reads=[B_vblk[q], B_wss[w]], writes=[B_ps[b2]])
                S.op("dve", lambda e, g=g, b2=b2, o=o: e.scalar_tensor_tensor(
                    out=tmp[o], in0=psum[b2][:, :], scalar=vgain[:, g:g + 1], in1=biasr[:, g * 512:(g + 1) * 512],
                    op0=ALU.mult, op1=ALU.add), reads=[B_ps[b2], B_vg, B_bias], writes=[B_tmp[o]])
                S.op("dve", lambda e, o=o, gi2=gi2: e.tensor_tensor(out=tmp[o], in0=tmp[o], in1=u[q][gi2], op=ALU.mult),
                     reads=[B_tmp[o], B_u[q][gi2]], writes=[B_tmp[o]])
                S.op("dve", lambda e, o=o, gi2=gi2: e.tensor_tensor(out=yb[o], in0=tmp[o], in1=sg[q][gi2], op=ALU.mult),
                     reads=[B_tmp[o], B_sg[q][gi2]], writes=[B_yb[o]])
                S.dma("pool", yT_part[g * 128:(g + 1) * 128, tb * TB:(tb + 1) * TB], yb[o],
                      reads=[B_yb[o]], writes=[B_yT_part])

        inproj(p["win"], colgroups, on_tile, nwbuf=2)
        S.barrier()

    def mixer_B(p):
        AR.reset()
        NQT = NTB
        scale = 128.0 ** -0.5
        NH = NTT * HPC
        assert NH <= 512
        wf = AR.bf16(KD * HPC).rearrange("p (k h) -> p k h", k=KD)
        wf32 = AR.f32(KD * HPC).rearrange("p (k h) -> p k h", k=KD)
        B_wf32 = Buf("wf32")
        bft = AR.f32(HPC)
        cT = AR.f32(NH)
        cref = AR.f32(NH)
        mark = AR.off
        bfrep = AR.f32(NH)
        zt = AR.f32(NH)
        pp = [AR.f32(NH) for _ in range(2)]
        if AR.off - mark < 4 * (TB // 2) + 2 * TB + 2 * (TB // 2):
            AR.f32(4 * (TB // 2) + 2 * TB + 2 * (TB // 2) - (AR.off - mark))
        end1 = AR.off
        Bq = [AR.f32(2 * NTT).rearrange("p (a k) -> p a k", a=2) for _ in range(2)]
        B_Bq = [Buf("Bq0"), Buf("Bq1")]
        qT = AR.bf16(L)
        kT = AR.bf16(L)
        Vt = AR.bf16(L).rearrange("p (t d) -> p t d", d=128)
        sgT = AR.bf16(L)
        end2 = AR.off
        AR.off = mark
        pT = [AR.bf16(TB) for _ in range(4)]
        rs = AR.f32(TB)
        ot = AR.f32(TB)
        yb = [AR.bf16(TB) for _ in range(2)]
        assert AR.off <= end1
        AR.off = end2
        B_wf, B_bft, B_bfrep, B_zt, B_cT, B_cref, B_Bm = (Buf(n) for n in ("wf", "bft", "bfrep", "zt", "cT", "cref", "Bm_unused"))
        B_pp = [Buf("pp0"), Buf("pp1")]
        B_qT, B_kT, B_Vt, B_sgT = Buf("qT"), Buf("kT"), Buf("Vt"), Buf("sgT")
        B_pT = [Buf(f"pT{i}") for i in range(4)]
        B_rs, B_ot = Buf("rs"), Buf("ot")
        B_yb = [Buf("yb0"), Buf("yb1")]
        PSF = 7
        S.dma("sp", wf32, p["wf"].rearrange("(k p) h -> p k h", p=128), writes=[B_wf32])
        S.op("dve", lambda e: e.tensor_copy(out=wf, in_=wf32), reads=[B_wf32], writes=[B_wf])
        S.dma("sp", bft, p["bf"].partition_broadcast(128), writes=[B_bft])
        bfv = bfrep.rearrange("p (t h) -> p t h", h=HPC)
        S.op("dve", lambda e: e.tensor_copy(out=bfv[:, 0, :], in_=bft), reads=[B_bft], writes=[B_bfrep])
        d = 1
        while d < NTT:
            n = min(d, NTT - d)
            S.op("dve", lambda e, d=d, n=n: e.tensor_copy(out=bfv[:, d:d + n, :], in_=bfv[:, 0:n, :]),
                 reads=[B_bfrep], writes=[B_bfrep])
            d *= 2

        def extra_tb(tb, hTs, B_hTs):
            fns = []
            for a in range(4):
                ti = tb * 4 + a
                for k in range(KD):
                    fns.append(lambda e, k=k, a=a, ti=ti: e.matmul(
                        psum[PSF][:, ti * HPC:(ti + 1) * HPC], lhsT=hTs[:, k, a * 128:(a + 1) * 128], rhs=wf[:, k, :],
                        start=(k == 0), stop=(k == KD - 1)))
            S.group("pe", fns, reads=[B_hTs, B_wf], writes=[B_ps[PSF]])

        def forget_prep():
            S.op("dve", lambda e: e.tensor_tensor(out=zt, in0=psum[PSF][:, 0:NH], in1=bfrep, op=ALU.add),
                 reads=[B_ps[PSF], B_bfrep], writes=[B_zt])
            S.op("act", lambda e: e.activation(out=zt, in_=zt, func=AF.Exp, scale=-1.0), reads=[B_zt], writes=[B_zt])
            S.op("act", lambda e: e.activation(out=zt, in_=zt, func=AF.Ln, bias=1.0), reads=[B_zt], writes=[B_zt])
            S.op("dve", lambda e: e.tensor_scalar(out=zt, in0=zt, scalar1=-1.0, scalar2=None, op0=ALU.mult),
                 reads=[B_zt], writes=[B_zt])
            S.group("pe", [lambda e: e.matmul(psum[4][:, 0:NH], lhsT=cmask_f, rhs=zt, start=True, stop=True)],
                    reads=[B_zt, B_const], writes=[B_ps[4]])
            S.group("pe", [lambda e: e.matmul(psum[5][:, 0:NH], lhsT=ones_f, rhs=zt, start=True, stop=True)],
                    reads=[B_zt, B_const], writes=[B_ps[5]])
            S.op("dve", lambda e: e.tensor_copy(out=pp[0], in_=psum[5][:, 0:NH]), reads=[B_ps[5]], writes=[B_pp[0]])
            cur = 0
            d = 1
            while d < NTT:
                nxt = 1 - cur
                S.op("dve", lambda e, cur=cur, nxt=nxt, d=d: e.tensor_copy(out=pp[nxt][:, 0:d * HPC], in_=pp[cur][:, 0:d * HPC]),
                     reads=[B_pp[cur]], writes=[B_pp[nxt]])
                S.op("dve", lambda e, cur=cur, nxt=nxt, d=d: e.tensor_tensor(
                    out=pp[nxt][:, d * HPC:NH], in0=pp[cur][:, d * HPC:NH], in1=pp[cur][:, 0:NH - d * HPC], op=ALU.add),
                    reads=[B_pp[cur]], writes=[B_pp[nxt]])
                cur = nxt
                d *= 2
            S.op("dve", lambda e, cur=cur: e.tensor_copy(out=cref, in_=pp[cur]), reads=[B_pp[cur]], writes=[B_cref])
            S.op("dve", lambda e: e.tensor_tensor(out=cT, in0=cref, in1=psum[5][:, 0:NH], op=ALU.subtract),
                 reads=[B_cref, B_ps[5]], writes=[B_cT])
            S.op("dve", lambda e: e.tensor_tensor(out=cT, in0=cT, in1=psum[4][:, 0:NH], op=ALU.add),
                 reads=[B_cT, B_ps[4]], writes=[B_cT])
            S.barrier()

        colgroups = [dict(c0=h * 512, width=512, orient="FFTF") for h in range(HPC)]

        def on_tile(h, tb, cc, bk):
            sl = slice(tb * TB, (tb + 1) * TB)
            if cc == 0:
                S.op("act", lambda e: e.copy(out=qT[:, sl], in_=psum[bk][:, :]), reads=[B_ps[bk]], writes=[B_qT])
            elif cc == 1:
                S.op("dve", lambda e: e.tensor_copy(out=kT[:, sl], in_=psum[bk][:, :]), reads=[B_ps[bk]], writes=[B_kT])
            elif cc == 2:
                S.op("dve", lambda e: e.tensor_copy(out=Vt[:, tb * 4:tb * 4 + 4, :],
                                                    in_=psum[bk][:, :].rearrange("p (a d) -> p a d", d=128)),
                     reads=[B_ps[bk]], writes=[B_Vt])
            else:
                S.op("act", lambda e: e.activation(out=sgT[:, sl], in_=psum[bk][:, :], func=AF.Silu),
                     reads=[B_ps[bk]], writes=[B_sgT])

        crefv = cref.rearrange("p (t h) -> p t h", h=HPC)
        cTv = cT.rearrange("p (t h) -> p t h", h=HPC)

        def attention(h):
            if h == 0:
                forget_prep()
            steps = []
            for qt in range(NQT):
                for kt in range(4 * qt + 4):
                    steps.append((qt, kt))

            def prep(qt):
                par = qt % 2
                nk = 4 * qt + 4
                for a in range(2):
                    R = crefv[:, 4 * qt + 2 * a + 1, h:h + 1]
                    S.op("dve", lambda e, a=a, R=R: e.tensor_scalar(out=Bq[par][:, a, 0:nk], in0=cTv[:, 0:nk, h], scalar1=R,
                                                                    scalar2=-1.0, op0=ALU.subtract, op1=ALU.mult),
                         reads=[B_cref, B_cT], writes=[B_Bq[par]])

            def issue_S(idx):
                qt, kt = steps[idx]
                if kt == 0:
                    prep(qt)
                i = kt - 4 * qt
                q0 = max(i, 0)
                bk = 2 + idx % 2
                S.group("pe", [lambda e: e.matmul(psum[bk][:, q0 * 128:TB], lhsT=kT[:, kt * 128:(kt + 1) * 128],
                                                   rhs=qT[:, qt * TB + q0 * 128:(qt + 1) * TB], start=True, stop=True)],
                        reads=[B_kT, B_qT], writes=[B_ps[bk]])

            def step(idx, qt, kt):
                i = kt - 4 * qt
                q0 = max(i, 0)
                bk = 2 + idx % 2
                sl = idx % 4
                par = qt % 2
                ob, sb = 4 + par, 6 + par
                nk = 4 * qt + 4
                for a in range(2):
                    c0_, c1_ = max(q0, 2 * a) * 128, (2 * a + 2) * 128
                    if c0_ >= c1_:
                        continue
                    S.op("act", lambda e, a=a, c0_=c0_, c1_=c1_: e.activation(
                        out=pT[sl][:, c0_:c1_], in_=psum[bk][:, c0_:c1_], func=AF.Exp,
                        bias=Bq[par][:, a, kt:kt + 1], scale=scale),
                        reads=[B_ps[bk], B_Bq[par]], writes=[B_pT[sl]])
                if i >= 0:
                    S.op("dve", lambda e: e.tensor_tensor(out=pT[sl][:, i * 128:(i + 1) * 128],
                                                          in0=pT[sl][:, i * 128:(i + 1) * 128], in1=cmask_b, op=ALU.mult),
                         reads=[B_pT[sl], B_const], writes=[B_pT[sl]])
                S.group("pe", [
                    lambda e: e.matmul(psum[ob][:, q0 * 128:TB], lhsT=Vt[:, kt, :], rhs=pT[sl][:, q0 * 128:TB],
                                       start=(kt == 0), stop=(kt == nk - 1), skip_group_check=True),
                    lambda e: e.matmul(psum[sb][:, q0 * 128:TB], lhsT=ones_b, rhs=pT[sl][:, q0 * 128:TB],
                                       start=(kt == 0), stop=(kt == nk - 1), skip_group_check=True)],
                    reads=[B_Vt, B_pT[sl], B_const], writes=[B_ps[ob], B_ps[sb]])
                if kt == nk - 1:
                    o = qt % 2
                    S.op("dve", lambda e: e.reciprocal(out=rs, in_=psum[sb][:, :]), reads=[B_ps[sb]], writes=[B_rs])
                    S.op("dve", lambda e: e.tensor_tensor(out=ot, in0=psum[ob][:, :], in1=rs, op=ALU.mult),
                         reads=[B_ps[ob], B_rs], writes=[B_ot])
                    S.op("dve", lambda e: e.tensor_tensor(out=yb[o], in0=ot, in1=sgT[:, qt * TB:(qt + 1) * TB], op=ALU.mult),
                         reads=[B_ot, B_sgT], writes=[B_yb[o]])
                    S.dma("pool", yT_part[h * 128:(h + 1) * 128, qt * TB:(qt + 1) * TB], yb[o],
                          reads=[B_yb[o]], writes=[B_yT_part])

            issue_S(0)
            for idx, (qt, kt) in enumerate(steps):
                if idx + 1 < len(steps):
                    issue_S(idx + 1)
                step(idx, qt, kt)

        inproj(p["win"], colgroups, on_tile, post_group=attention, extra_tb=extra_tb)
        S.barrier()

    def outproj(p):
        AR.reset()
        Wo = AR.bf16(KE * FCR).rearrange("p (k c) -> p k c", k=KE)
        NP = 4 if KE % 4 == 0 else 1
        KP = KE // NP
        NSL = 4
        ysl = [AR.bf16(KP * TB).rearrange("p (k t) -> p k t", k=KP) for _ in range(NSL)]
        xb = [AR.f32(FC * TB).rearrange("p (j t) -> p j t", j=FC) for _ in range(2)]
        xn = [AR.f32(FC * TB).rearrange("p (j t) -> p j t", j=FC) for _ in range(2)]
        KPS = min(4, KE)
        stg = [AR.f32(KPS * FCR).rearrange("p (k c) -> p k c", k=KPS) for _ in range(2)]
        B_stg = [Buf("ostg0"), Buf("ostg1")]
        B_Wo = Buf("Wo")
        B_ysl = [Buf(f"ysl{i}") for i in range(NSL)]
        B_xb = [Buf("xb0"), Buf("xb1")]
        B_xn = [Buf("xn0"), Buf("xn1")]
        for q in range(KE // KPS):
            sp_ = q % 2
            S.dma("sp", stg[sp_], p["wout"][:, q * KPS * FCR:(q + 1) * KPS * FCR].rearrange("p (k c) -> p k c", k=KPS),
                  writes=[B_stg[sp_]])
            S.op("dve", lambda e, q=q, sp_=sp_: e.tensor_copy(out=Wo[:, q * KPS:(q + 1) * KPS, :], in_=stg[sp_]),
                 reads=[B_stg[sp_]], writes=[B_Wo])
        scnt = 0
        for tb in range(NTB):
            par = tb % 2
            S.dma("sp", xb[par], xblk(xres, tb), reads=[B_xres[tb]], writes=[B_xb[par]])
            for pc in range(NP):
                sl = scnt % NSL
                scnt += 1
                k0, k1 = pc * KP, (pc + 1) * KP
                r0 = (tb * NP + pc) * 128
                S.dma("sp", ysl[sl], yT_full[r0:r0 + 128, :].rearrange("p (k t) -> p k t", k=KP),
                      reads=[B_yT_full], writes=[B_ysl[sl]])
                for cc in range(FC):
                    bk = par * 4 + cc
                    fns = [lambda e, k=k, cc=cc, bk=bk, sl=sl, pc=pc: e.matmul(
                        psum[bk][:, :], lhsT=Wo[:, pc * KP + k, cc * 128:(cc + 1) * 128], rhs=ysl[sl][:, k, :],
                        start=(pc == 0 and k == 0), stop=(pc == NP - 1 and k == KP - 1), skip_group_check=True)
                        for k in range(KP)]
                    S.group("pe", fns, reads=[B_Wo, B_ysl[sl]], writes=[B_ps[bk]])
            for cc in range(FC):
                bk = par * 4 + cc
                S.op("dve", lambda e, cc=cc, bk=bk, par=par: e.tensor_tensor(
                    out=xn[par][:, cc, :], in0=psum[bk][:, :], in1=xb[par][:, cc, :], op=ALU.add),
                    reads=[B_ps[bk], B_xb[par]], writes=[B_xn[par]])
            S.dma("pool", xblk(xout, tb), xn[par], reads=[B_xn[par]], writes=[B_xout[tb]])
        S.barrier()

    if kind == "ss0":
        sumsq_pass(xres, B_xres)
    elif kind == "norm":
        norm_stage(p["norm"], to_output=False)
    elif kind == "final":
        norm_stage(p["norm"], to_output=True)
    elif kind == "mixA1":
        mixer_A1(p)
    elif kind == "mixA2":
        mixer_A2(p)
    elif kind == "mixB":
        mixer_B(p)
    elif kind == "mixC":
        mixer_C(p)
    elif kind == "out":
        outproj(p)
        sumsq_pass(xout, B_xout)
    S.barrier()

    with nc.Block() as block:
        S.emit(block)
    stack.close()
    return nc


def _pp(vec, n):
    return np.ascontiguousarray(np.asarray(vec, np.float32).reshape(n, 128).T)


def layer_inputs(cfg, li, c, inp):
    D, E, EC, HPC, FC, FCR = cfg.D, cfg.E, cfg.EC, cfg.HPC, cfg.FC, cfg.FCR
    m = MIX[li]
    pre = f"l{li}_"
    d = {}
    d["gain"] = _pp(inp[pre + "norm"][c * FCR:(c + 1) * FCR], FC)
    w_in = inp[pre + "w_in"]
    ch = slice(c * EC, (c + 1) * EC)
    if m == "A":
        u = w_in[:, 0 * E:1 * E][:, ch]
        v = w_in[:, 1 * E:2 * E][:, ch]
        g = w_in[:, 2 * E:3 * E][:, ch]
        cols = [v]
        for pr in range(HPC // 2):
            a, b = 2 * pr, 2 * pr + 1
            cols += [u[:, a * 128:(a + 1) * 128], u[:, b * 128:(b + 1) * 128],
                     g[:, a * 128:(a + 1) * 128], g[:, b * 128:(b + 1) * 128]]
        gwv = min(512, EC)
        groups = [v[:, i * gwv:(i + 1) * gwv] for i in range(EC // gwv)]
        for pr in range(HPC // 2):
            groups.append(np.concatenate(cols[1 + 4 * pr:5 + 4 * pr], axis=1))
        d["win"] = _blk_w(groups)
        d["vgain"] = _pp(inp[pre + "v_gain"][ch], HPC)
        ws = inp[pre + "ws"][c * HPC:(c + 1) * HPC]
        d["wsT"] = np.ascontiguousarray(np.transpose(ws, (2, 0, 1)).reshape(128, HPC * 128))
        d["wsb"] = np.ascontiguousarray(inp[pre + "ws_bias"][c * HPC:(c + 1) * HPC].reshape(1, HPC * 128))
    else:
        parts = [w_in[:, j * E:(j + 1) * E][:, ch] for j in range(4)]
        cols = []
        for h in range(HPC):
            for j in range(4):
                cols.append(parts[j][:, h * 128:(h + 1) * 128])
        d["win"] = _blk_w([np.concatenate(cols[4 * h:4 * h + 4], axis=1) for h in range(HPC)])
        if m == "B":
            d["wf"] = np.ascontiguousarray(w_in[:, 4 * E + c * HPC:4 * E + (c + 1) * HPC])
            d["bf"] = np.ascontiguousarray(inp[pre + "b_f"][c * HPC:(c + 1) * HPC].reshape(1, HPC))
        else:
            cw = inp[pre + "conv_w"][:, ch]
            d["convw"] = np.ascontiguousarray(
                np.transpose(cw.reshape(3, HPC, 128), (2, 1, 0)).reshape(128, HPC * 3))
    d["wout"] = _blk_w([inp[pre + "w_out"][:, c * FCR:(c + 1) * FCR]])
    return d


def _blk_w(groups):
    out = []
    for g in groups:
        K = g.shape[0] // 128
        out.append(np.transpose(g.reshape(K, 128, g.shape[1]), (1, 0, 2)).reshape(128, K * g.shape[1]))
    return np.ascontiguousarray(np.concatenate(out, axis=1))


def _blk_hT(cfg, parts):
    full = np.concatenate(parts, axis=0)
    a = full.reshape(cfg.KD, 128, cfg.NTB, TB)
    return np.ascontiguousarray(np.transpose(a, (2, 1, 0, 3)).reshape(cfg.NTB * 128, cfg.KD * TB))


def _blk_yT(cfg, parts):
    full = np.concatenate(parts, axis=0)
    KP = cfg.KE // 4
    a = full.reshape(4, KP, 128, cfg.NTB, TB)
    return np.ascontiguousarray(np.transpose(a, (3, 0, 2, 1, 4)).reshape(cfg.NTB * 4 * 128, KP * TB))


_CONST = None


def consts():
    global _CONST
    if _CONST is None:
        sel = np.zeros((128, 2), np.float32)
        sel[0, 0] = 1.0
        sel[1, 1] = 1.0
        _CONST = {"cmask": np.triu(np.ones((128, 128), np.float32)), "ident": np.eye(128, dtype=np.float32), "sel": sel}
    return _CONST


_PROGS = {}


def launch(cfg, kind, per_core):
    key = (cfg.L, cfg.D, cfg.E, kind)
    if key not in _PROGS:
        _PROGS[key] = build_segment(cfg, kind)
    nc = _PROGS[key]
    in_maps = []
    for c in range(NCORES):
        m = dict(per_core[c])
        m.update(consts())
        in_maps.append(m)
    res = run_bass_kernel_spmd(nc, in_maps, core_ids=list(range(NCORES)))
    return res.results


def run_forward(cfg, inp, layers=(0, 1, 2, 3)):
    inp = {k: np.asarray(v) for k, v in inp.items()}
    x = inp["x"][0]
    FCR = cfg.FCR
    C = range(NCORES)
    xT = [np.ascontiguousarray(x[:, c * FCR:(c + 1) * FCR].T) for c in C]
    r = launch(cfg, "ss0", [{"xT": xT[c]} for c in C])
    ss_all = np.ascontiguousarray(np.concatenate([r[c]["ss_part"] for c in C], axis=0))
    for li in layers:
        lp = [layer_inputs(cfg, li, c, inp) for c in C]
        r = launch(cfg, "norm", [{"xT": xT[c], "ss_all": ss_all, "gain": lp[c]["gain"]} for c in C])
        hT_full = _blk_hT(cfg, [r[c]["hT_part"] for c in C])
        m = MIX[li]
        if m == "A":
            r = launch(cfg, "mixA1", [{"hT_blk": hT_full, "win": lp[c]["win"]} for c in C])
            ssv_all = np.ascontiguousarray(np.concatenate([r[c]["ssv_part"] for c in C], axis=0))
            vraw = [r[c]["vraw"] for c in C]
            r = launch(cfg, "mixA2", [{"hT_blk": hT_full, "win": lp[c]["win"], "ssv_all": ssv_all, "vraw": vraw[c],
                                       "vgain": lp[c]["vgain"], "wsT": lp[c]["wsT"], "wsb": lp[c]["wsb"]} for c in C])
            del vraw
        elif m == "B":
            r = launch(cfg, "mixB", [{"hT_blk": hT_full, "win": lp[c]["win"], "wf": lp[c]["wf"], "bf": lp[c]["bf"]}
                                     for c in C])
        else:
            r = launch(cfg, "mixC", [{"hT_blk": hT_full, "win": lp[c]["win"], "convw": lp[c]["convw"]} for c in C])
        del hT_full
        yT_full = _blk_yT(cfg, [r[c]["yT_part"] for c in C])
        r = launch(cfg, "out", [{"yT_blk": yT_full, "xT": xT[c], "wout": lp[c]["wout"]} for c in C])
        del yT_full, lp
        xT = [np.ascontiguousarray(r[c]["xT_out"]) for c in C]
        ss_all = np.ascontiguousarray(np.concatenate([r[c]["ss_part"] for c in C], axis=0))
    r = launch(cfg, "final", [{"xT": xT[c], "ss_all": ss_all,
                               "gain": _pp(inp["final_norm"][c * FCR:(c + 1) * FCR], cfg.FC)} for c in C])
    out = np.concatenate([r[c]["out"] for c in C], axis=1)
    return out[None].astype(np.float32)


def kernel(**inputs):
    return run_forward(Cfg(), inputs)
```
